# Optimizing a Trainium2 kernel written in Bass

```python
import math
import jax
import jax.numpy as jnp
from jax import lax
import numpy as np

D_MODEL = 1024
BATCH = 16
SEQ = 2048
DEPTH = 2

GRID_W = 64
CTX_LEN = 256
N_MOD = 9
FFN_HIDDEN = 2816
NORM_EPS = 1e-6
ROPE_THETA = 10000.0
Q_BLOCK = 128

DIFF_HEADS = 4
DIFF_HD = 64
MLA_HEADS = 4
MLA_Q_RANK = 256
MLA_KV_RANK = 128
MLA_NOPE = 128
MLA_ROPE = 64
MLA_V = 128
MLA_SCALE = (MLA_NOPE + MLA_ROPE) ** -0.5

SSD_HEADS = 16
SSD_HD = 64
SSD_INNER = SSD_HEADS * SSD_HD
SSD_GROUPS = 4
SSD_STATE = 128
SSD_CONV = 5
SSD_CHUNK = 128

NA_HEADS = 8
NA_HD = 64
NA_WIN_R = 8
NA_WIN_C = 16

DIFF_W = DIFF_HEADS * 2 * DIFF_HD
MLA_OUT = MLA_HEADS * MLA_V
AB_SIZES = (DIFF_W, DIFF_W, DIFF_W, MLA_Q_RANK, MLA_KV_RANK, MLA_ROPE)
IN_AB = DIFF_W * 3 + MLA_Q_RANK + MLA_KV_RANK + MLA_ROPE
MIX_AB = DIFF_W + MLA_OUT
SSD_CONV_CH = SSD_INNER + 2 * SSD_GROUPS * SSD_STATE
NA_W = NA_HEADS * NA_HD
CD_SIZES = (SSD_INNER, SSD_CONV_CH, 2 * SSD_HEADS, NA_W, NA_W, NA_W)
IN_CD = SSD_INNER + SSD_CONV_CH + 2 * SSD_HEADS + 3 * NA_W
MIX_CD = SSD_INNER + NA_W

kernel_name = 'hybrid_diffattn_mla_ssd_natten_dit'


def _split(p, sizes):
    return jnp.split(p, [int(s) for s in np.cumsum(sizes)[:-1]], axis=-1)


def rms_norm(x, g):
    xf = x.astype(jnp.float32)
    y = xf * lax.rsqrt(jnp.mean(xf * xf, axis=-1, keepdims=True) + NORM_EPS)
    return (y * g.astype(jnp.float32)).astype(x.dtype)


def ada_norm(x, g, shift, scale):
    return rms_norm(x, g) * (1.0 + scale) + shift


def swiglu(h, w_in, w_out):
    gate, up = jnp.split(h @ w_in, 2, axis=-1)
    return (jax.nn.silu(gate) * up) @ w_out


def axial_rope_table(n_tok, dim, dtype):
    t = jnp.arange(n_tok)
    row = (t // GRID_W).astype(jnp.float32)
    col = (t % GRID_W).astype(jnp.float32)
    nf = dim // 4
    freqs = ROPE_THETA ** (-jnp.arange(nf, dtype=jnp.float32) / nf)
    ang_r = row[:, None] * freqs
    ang_c = col[:, None] * freqs
    ang = jnp.concatenate([ang_r, ang_r, ang_c, ang_c], axis=-1)
    return jnp.cos(ang).astype(dtype), jnp.sin(ang).astype(dtype)


def apply_rope(x, cos, sin):
    nf = x.shape[-1] // 4
    xs = x.reshape(x.shape[:-1] + (2, 2, nf))
    rot = jnp.stack([-xs[..., 1, :], xs[..., 0, :]], axis=-2).reshape(x.shape)
    bshape = (x.shape[1],) + (1,) * (x.ndim - 3) + (x.shape[-1],)
    return x * cos.reshape(bshape) + rot * sin.reshape(bshape)


def plain_attend(q, k, v, scale):
    s = jnp.einsum('bqhd,bkhd->bhqk', q, k).astype(jnp.float32) * scale
    p = jax.nn.softmax(s, axis=-1).astype(v.dtype)
    return jnp.einsum('bhqk,bkhe->bqhe', p, v)


def diff_attend(q, k, v, lam):
    s = jnp.einsum('bqhcd,bkhcd->bhcqk', q, k).astype(jnp.float32) * (DIFF_HD ** -0.5)
    p = jax.nn.softmax(s, axis=-1)
    a = (p[:, :, 0] - lam * p[:, :, 1]).astype(v.dtype)
    return jnp.einsum('bhqk,bkhe->bqhe', a, v)


def over_query_blocks(attend, q):
    b, s = q.shape[:2]
    qb = q.reshape((b, s // Q_BLOCK, Q_BLOCK) + q.shape[2:]).swapaxes(0, 1)
    ob = lax.map(attend, qb).swapaxes(0, 1)
    return ob.reshape((b, s) + ob.shape[3:])


def mla_heads(c_q, c_kv, k_rope, g_q, w_uq, g_kv, w_ukv, cos, sin):
    lead = c_q.shape[:-1]
    q = (rms_norm(c_q, g_q) @ w_uq).reshape(lead + (MLA_HEADS, MLA_NOPE + MLA_ROPE))
    kv = (rms_norm(c_kv, g_kv) @ w_ukv).reshape(lead + (MLA_HEADS, MLA_NOPE + MLA_V))
    q_nope, q_rope = jnp.split(q, [MLA_NOPE], axis=-1)
    k_nope, v = jnp.split(kv, [MLA_NOPE], axis=-1)
    if cos is not None:
        q_rope = apply_rope(q_rope, cos, sin)
        k_rope = apply_rope(k_rope, cos, sin)
    k_rope = jnp.broadcast_to(k_rope[..., None, :], lead + (MLA_HEADS, MLA_ROPE))
    q = jnp.concatenate([q_nope, q_rope], axis=-1)
    k = jnp.concatenate([k_nope, k_rope], axis=-1)
    return q, k, v


def mixer_ab(hx, hc, w_in, lam_q1, lam_k1, lam_q2, lam_k2, g_sub, g_q, w_uq, g_kv, w_ukv, w_out,
             lambda_init, with_ctx):
    f32 = jnp.float32
    n_lat = hx.shape[1]
    cos_d, sin_d = axial_rope_table(n_lat, DIFF_HD, hx.dtype)
    cos_m, sin_m = axial_rope_table(n_lat, MLA_ROPE, hx.dtype)
    lam = (jnp.exp(jnp.sum(lam_q1.astype(f32) * lam_k1.astype(f32)))
           - jnp.exp(jnp.sum(lam_q2.astype(f32) * lam_k2.astype(f32))) + lambda_init)

    def project(h, cd, sd, cm, sm):
        qd, kd, vd, cq, ckv, kr = _split(h @ w_in, AB_SIZES)
        lead = h.shape[:-1]
        qd = qd.reshape(lead + (DIFF_HEADS, 2, DIFF_HD))
        kd = kd.reshape(lead + (DIFF_HEADS, 2, DIFF_HD))
        vd = vd.reshape(lead + (DIFF_HEADS, 2 * DIFF_HD))
        if cd is not None:
            qd = apply_rope(qd, cd, sd)
            kd = apply_rope(kd, cd, sd)
        qm, km, vm = mla_heads(cq, ckv, kr, g_q, w_uq, g_kv, w_ukv, cm, sm)
        return qd, kd, vd, qm, km, vm

    qd_x, kd_x, vd_x, qm_x, km_x, vm_x = project(hx, cos_d, sin_d, cos_m, sin_m)
    qd_c, kd_c, vd_c, qm_c, km_c, vm_c = project(hc, None, None, None, None)

    def merge(od, om):
        od = rms_norm(od, g_sub) * (1.0 - lambda_init)
        lead = od.shape[:2]
        cat = jnp.concatenate([od.reshape(lead + (DIFF_W,)), om.reshape(lead + (MLA_OUT,))], axis=-1)
        return cat @ w_out

    kd_all = jnp.concatenate([kd_x, kd_c], axis=1)
    vd_all = jnp.concatenate([vd_x, vd_c], axis=1)
    km_all = jnp.concatenate([km_x, km_c], axis=1)
    vm_all = jnp.concatenate([vm_x, vm_c], axis=1)
    od_x = over_query_blocks(lambda qb: diff_attend(qb, kd_all, vd_all, lam), qd_x)
    om_x = over_query_blocks(lambda qb: plain_attend(qb, km_all, vm_all, MLA_SCALE), qm_x)
    out_x = merge(od_x, om_x)
    out_c = None
    if with_ctx:
        out_c = merge(diff_attend(qd_c, kd_c, vd_c, lam), plain_attend(qm_c, km_c, vm_c, MLA_SCALE))
    return out_x, out_c


def dwconv_centred(x, w, b):
    y = lax.conv_general_dilated(
        x, w[:, None, :].astype(x.dtype), window_strides=(1,),
        padding=[(SSD_CONV // 2, SSD_CONV // 2)],
        dimension_numbers=('NWC', 'WIO', 'NWC'), feature_group_count=x.shape[-1])
    return y + b.astype(x.dtype)


def ssd_scan(x, dt, a, bm, cm, d_skip, h0):
    f32 = jnp.float32
    bsz, n, nh, hd = x.shape
    ng, ns = bm.shape[-2:]
    nr = nh // ng
    nc = n // SSD_CHUNK
    cl = SSD_CHUNK
    xf = x.astype(f32)
    xd = (xf * dt[..., None]).reshape(bsz, nc, cl, ng, nr, hd)
    bb = bm.astype(f32).reshape(bsz, nc, cl, ng, ns)
    cc = cm.astype(f32).reshape(bsz, nc, cl, ng, ns)
    a_dt = (dt * a).reshape(bsz, nc, cl, ng, nr).transpose(0, 3, 4, 1, 2)
    a_cs = jnp.cumsum(a_dt, axis=-1)
    tril = jnp.tril(jnp.ones((cl, cl), dtype=bool))
    decay = jnp.exp(jnp.where(tril, a_cs[..., :, None] - a_cs[..., None, :], -jnp.inf))
    cb = jnp.einsum('bclgn,bcsgn->bgcls', cc, bb)
    y_diag = jnp.einsum('bgcls,bgrcls,bcsgrp->bclgrp', cb, decay, xd)
    decay_states = jnp.exp(a_cs[..., -1:] - a_cs)
    states = jnp.einsum('bclgn,bgrcl,bclgrp->bcgrpn', bb, decay_states, xd)
    states = jnp.concatenate([h0.reshape(bsz, 1, ng, nr, hd, ns), states], axis=1)
    tot = jnp.cumsum(jnp.pad(a_cs[..., -1], ((0, 0), (0, 0), (0, 0), (1, 0))), axis=-1)
    tril_c = jnp.tril(jnp.ones((nc + 1, nc + 1), dtype=bool))
    decay_chunk = jnp.exp(jnp.where(tril_c, tot[..., :, None] - tot[..., None, :], -jnp.inf))
    states = jnp.einsum('bgrzc,bcgrpn->bzgrpn', decay_chunk, states)
    y_off = jnp.einsum('bclgn,bcgrpn,bgrcl->bclgrp', cc, states[:, :-1], jnp.exp(a_cs))
    y = (y_diag + y_off).reshape(bsz, n, nh, hd) + d_skip.astype(f32)[:, None] * xf
    return y.astype(x.dtype), states[:, -1].reshape(bsz, nh, hd, ns)


def bidir_ssd(xs, bm, cm, dt, a, d_skip, h0_f, h0_b):
    flip = lambda t: jnp.flip(t, axis=1)
    y_f, h_f = ssd_scan(xs, dt[..., 0, :], a[0], bm, cm, d_skip[0], h0_f)
    y_b, h_b = ssd_scan(flip(xs), flip(dt[..., 1, :]), a[1], flip(bm), flip(cm), d_skip[1], h0_b)
    return y_f + flip(y_b), h_f, h_b


def gated_rms_norm(y, z, g):
    yz = y * jax.nn.silu(z)
    shp = yz.shape[:-1] + (SSD_GROUPS, SSD_INNER // SSD_GROUPS)
    return rms_norm(yz.reshape(shp), g.reshape(SSD_GROUPS, SSD_INNER // SSD_GROUPS)).reshape(yz.shape)


def na_attention(q, k, v, k_ctx, v_ctx, rpb):
    b, s, h, d = q.shape
    rows = s // GRID_W
    kr = min(NA_WIN_R, rows)
    kc = min(NA_WIN_C, GRID_W)
    scale = d ** -0.5
    qcol = jnp.arange(GRID_W)
    key_cols = jnp.clip(qcol - kc // 2, 0, GRID_W - kc)[:, None] + jnp.arange(kc)
    dc = key_cols - qcol[:, None] + (NA_WIN_C - 1)
    k_grid = k.reshape(b, rows, GRID_W, h, d)
    v_grid = v.reshape(b, rows, GRID_W, h, d)
    q_rows = q.reshape(b, rows, GRID_W, h, d).swapaxes(0, 1)
    bias_tab = rpb.astype(jnp.float32)

    def one_row(args):
        r, q_r = args
        r0 = jnp.clip(r - kr // 2, 0, rows - kr)
        dr = r0 + jnp.arange(kr) - r + (NA_WIN_R - 1)
        k_win = lax.dynamic_slice_in_dim(k_grid, r0, kr, axis=1)[:, :, key_cols]
        v_win = lax.dynamic_slice_in_dim(v_grid, r0, kr, axis=1)[:, :, key_cols]
        bias = bias_tab[:, dr[:, None, None], dc[None]].transpose(0, 2, 1, 3)
        s_lat = jnp.einsum('bqhd,bjqmhd->bhqjm', q_r, k_win).astype(jnp.float32) * scale + bias
        s_lat = s_lat.reshape(b, h, GRID_W, kr * kc)
        s_ctx = jnp.einsum('bqhd,bkhd->bhqk', q_r, k_ctx).astype(jnp.float32) * scale
        p = jax.nn.softmax(jnp.concatenate([s_lat, s_ctx], axis=-1), axis=-1).astype(v.dtype)
        p_lat = p[..., :kr * kc].reshape(b, h, GRID_W, kr, kc)
        p_ctx = p[..., kr * kc:]
        return (jnp.einsum('bhqjm,bjqmhd->bqhd', p_lat, v_win)
                + jnp.einsum('bhqk,bkhd->bqhd', p_ctx, v_ctx))

    o = lax.map(one_row, (jnp.arange(rows), q_rows))
    return o.swapaxes(0, 1).reshape(b, s, h, d)


def mixer_cd(hx, hc, w_in, conv_w, conv_b, dt_bias, a_log, d_skip, g_norm, rpb, w_out, with_ctx):
    a = -jnp.exp(a_log.astype(jnp.float32))

    def project(h):
        z, xbc, dt_raw, q, k, v = _split(h @ w_in, CD_SIZES)
        lead = h.shape[:-1]
        xbc = jax.nn.silu(dwconv_centred(xbc, conv_w, conv_b))
        xs, bm, cm = _split(xbc, (SSD_INNER, SSD_GROUPS * SSD_STATE, SSD_GROUPS * SSD_STATE))
        dt = jax.nn.softplus(dt_raw.astype(jnp.float32).reshape(lead + (2, SSD_HEADS))
                             + dt_bias.astype(jnp.float32))
        heads = lambda t: t.reshape(lead + (NA_HEADS, NA_HD))
        return (z, xs.reshape(lead + (SSD_HEADS, SSD_HD)),
                bm.reshape(lead + (SSD_GROUPS, SSD_STATE)), cm.reshape(lead + (SSD_GROUPS, SSD_STATE)),
                dt, heads(q), heads(k), heads(v))

    z_x, xs_x, b_x, c_x, dt_x, q_x, k_x, v_x = project(hx)
    z_c, xs_c, b_c, c_c, dt_c, q_c, k_c, v_c = project(hc)
    h0 = jnp.zeros((hc.shape[0], SSD_HEADS, SSD_HD, SSD_STATE), jnp.float32)
    y_c, hf_c, hb_c = bidir_ssd(xs_c, b_c, c_c, dt_c, a, d_skip, h0, h0)
    y_x, _, _ = bidir_ssd(xs_x, b_x, c_x, dt_x, a, d_skip, hf_c, hb_c)

    def merge(y, z, o):
        lead = y.shape[:2]
        ys = gated_rms_norm(y.reshape(lead + (SSD_INNER,)), z, g_norm)
        return jnp.concatenate([ys, o.reshape(lead + (NA_W,))], axis=-1) @ w_out

    out_x = merge(y_x, z_x, na_attention(q_x, k_x, v_x, k_c, v_c, rpb))
    out_c = None
    if with_ctx:
        out_c = merge(y_c, z_c, plain_attend(q_c, k_c, v_c, NA_HD ** -0.5))
    return out_x, out_c


def setup_inputs(seed: int = 0) -> dict:
    key = jax.random.key(seed)
    ks = iter(jax.random.split(key, 48))
    f32 = jnp.float32
    n_even = (DEPTH + 1) // 2
    n_odd = DEPTH // 2

    def normal(shape):
        return jax.random.normal(next(ks), shape, f32)

    def dense(shape, fan_in, gain=1.0):
        return gain * fan_in ** -0.5 * normal(shape)

    def norm_gain(shape):
        return 1.0 + 0.02 * normal(shape)

    x = normal((BATCH, SEQ, D_MODEL))
    c = normal((BATCH, D_MODEL))
    ctx = normal((BATCH, CTX_LEN, D_MODEL))
    c_ctx = normal((D_MODEL,))
    w_mod = dense((DEPTH, D_MODEL, N_MOD * D_MODEL), D_MODEL, 0.5)
    b_mod = 0.02 * normal((DEPTH, N_MOD * D_MODEL))
    g_ffn1 = norm_gain((DEPTH, D_MODEL))
    w_ffn1_in = dense((DEPTH, D_MODEL, 2 * FFN_HIDDEN), D_MODEL)
    w_ffn1_out = dense((DEPTH, FFN_HIDDEN, D_MODEL), FFN_HIDDEN)
    g_mix = norm_gain((DEPTH, D_MODEL))
    g_ffn2 = norm_gain((DEPTH, D_MODEL))
    w_ffn2_in = dense((DEPTH, D_MODEL, 2 * FFN_HIDDEN), D_MODEL)
    w_ffn2_out = dense((DEPTH, FFN_HIDDEN, D_MODEL), FFN_HIDDEN)
    ab_w_in = dense((n_even, D_MODEL, IN_AB), D_MODEL)
    ab_lam_q1 = 0.1 * normal((n_even, DIFF_HD))
    ab_lam_k1 = 0.1 * normal((n_even, DIFF_HD))
    ab_lam_q2 = 0.1 * normal((n_even, DIFF_HD))
    ab_lam_k2 = 0.1 * normal((n_even, DIFF_HD))
    ab_g_subln = norm_gain((n_even, 2 * DIFF_HD))
    ab_g_q = norm_gain((n_even, MLA_Q_RANK))
    ab_w_uq = dense((n_even, MLA_Q_RANK, MLA_HEADS * (MLA_NOPE + MLA_ROPE)), MLA_Q_RANK)
    ab_g_kv = norm_gain((n_even, MLA_KV_RANK))
    ab_w_ukv = dense((n_even, MLA_KV_RANK, MLA_HEADS * (MLA_NOPE + MLA_V)), MLA_KV_RANK)
    ab_w_out = dense((n_even, MIX_AB, D_MODEL), MIX_AB)
    cd_w_in = dense((n_odd, D_MODEL, IN_CD), D_MODEL)
    cd_conv_w = dense((n_odd, SSD_CONV, SSD_CONV_CH), SSD_CONV)
    cd_conv_b = 0.02 * normal((n_odd, SSD_CONV_CH))
    dt0 = jnp.exp(jax.random.uniform(next(ks), (n_odd, 2, SSD_HEADS), f32,
                                     minval=math.log(1e-3), maxval=math.log(1e-1)))
    cd_dt_bias = dt0 + jnp.log(-jnp.expm1(-dt0))
    cd_a_log = jnp.log(jax.random.uniform(next(ks), (n_odd, 2, SSD_HEADS), f32, minval=1.0, maxval=16.0))
    cd_d_skip = 1.0 + 0.1 * normal((n_odd, 2, SSD_HEADS))
    cd_g_norm = norm_gain((n_odd, SSD_INNER))
    cd_rpb = 0.05 * normal((n_odd, NA_HEADS, 2 * NA_WIN_R - 1, 2 * NA_WIN_C - 1))
    cd_w_out = dense((n_odd, MIX_CD, D_MODEL), MIX_CD)
    g_final = norm_gain((D_MODEL,))
    return {'x': x, 'c': c, 'ctx': ctx, 'c_ctx': c_ctx, 'w_mod': w_mod, 'b_mod': b_mod,
            'g_ffn1': g_ffn1, 'w_ffn1_in': w_ffn1_in, 'w_ffn1_out': w_ffn1_out, 'g_mix': g_mix,
            'g_ffn2': g_ffn2, 'w_ffn2_in': w_ffn2_in, 'w_ffn2_out': w_ffn2_out,
            'ab_w_in': ab_w_in, 'ab_lam_q1': ab_lam_q1, 'ab_lam_k1': ab_lam_k1,
            'ab_lam_q2': ab_lam_q2, 'ab_lam_k2': ab_lam_k2, 'ab_g_subln': ab_g_subln,
            'ab_g_q': ab_g_q, 'ab_w_uq': ab_w_uq, 'ab_g_kv': ab_g_kv, 'ab_w_ukv': ab_w_ukv,
            'ab_w_out': ab_w_out, 'cd_w_in': cd_w_in, 'cd_conv_w': cd_conv_w, 'cd_conv_b': cd_conv_b,
            'cd_dt_bias': cd_dt_bias, 'cd_a_log': cd_a_log, 'cd_d_skip': cd_d_skip,
            'cd_g_norm': cd_g_norm, 'cd_rpb': cd_rpb, 'cd_w_out': cd_w_out, 'g_final': g_final}


def reference(x, c, ctx, c_ctx, w_mod, b_mod, g_ffn1, w_ffn1_in, w_ffn1_out, g_mix, g_ffn2,
              w_ffn2_in, w_ffn2_out, ab_w_in, ab_lam_q1, ab_lam_k1, ab_lam_q2, ab_lam_k2,
              ab_g_subln, ab_g_q, ab_w_uq, ab_g_kv, ab_w_ukv, ab_w_out, cd_w_in, cd_conv_w,
              cd_conv_b, cd_dt_bias, cd_a_log, cd_d_skip, cd_g_norm, cd_rpb, cd_w_out, g_final):
    silu_c = jax.nn.silu(c)
    silu_cc = jax.nn.silu(c_ctx)
    for i in range(DEPTH):
        last = i == DEPTH - 1
        mod_x = (silu_c @ w_mod[i] + b_mod[i]).reshape(c.shape[0], 1, N_MOD, D_MODEL)
        mod_c = (silu_cc @ w_mod[i] + b_mod[i]).reshape(N_MOD, D_MODEL)
        mx = [mod_x[:, :, j] for j in range(N_MOD)]
        mc = [mod_c[j] for j in range(N_MOD)]
        x = x + 0.5 * mx[2] * swiglu(ada_norm(x, g_ffn1[i], mx[0], mx[1]), w_ffn1_in[i], w_ffn1_out[i])
        ctx = ctx + 0.5 * mc[2] * swiglu(ada_norm(ctx, g_ffn1[i], mc[0], mc[1]), w_ffn1_in[i], w_ffn1_out[i])
        hx = ada_norm(x, g_mix[i], mx[3], mx[4])
        hc = ada_norm(ctx, g_mix[i], mc[3], mc[4])
        j = i // 2
        if i % 2 == 0:
            ox, oc = mixer_ab(hx, hc, ab_w_in[j], ab_lam_q1[j], ab_lam_k1[j], ab_lam_q2[j], ab_lam_k2[j],
                              ab_g_subln[j], ab_g_q[j], ab_w_uq[j], ab_g_kv[j], ab_w_ukv[j], ab_w_out[j],
                              0.8 - 0.6 * math.exp(-0.3 * i), not last)
        else:
            ox, oc = mixer_cd(hx, hc, cd_w_in[j], cd_conv_w[j], cd_conv_b[j], cd_dt_bias[j], cd_a_log[j],
                              cd_d_skip[j], cd_g_norm[j], cd_rpb[j], cd_w_out[j], not last)
        x = x + mx[5] * ox
        x = x + 0.5 * mx[8] * swiglu(ada_norm(x, g_ffn2[i], mx[6], mx[7]), w_ffn2_in[i], w_ffn2_out[i])
        if not last:
            ctx = ctx + mc[5] * oc
            ctx = ctx + 0.5 * mc[8] * swiglu(ada_norm(ctx, g_ffn2[i], mc[6], mc[7]), w_ffn2_in[i], w_ffn2_out[i])
    return rms_norm(x, g_final)
```

```python
import numpy as np
import ml_dtypes
from contextlib import ExitStack
import concourse.bass as bass
import concourse.mybir as mybir
from concourse.bass_utils import run_bass_kernel_spmd

F32 = mybir.dt.float32
BF16 = mybir.dt.bfloat16
AF = mybir.ActivationFunctionType
ALU = mybir.AluOpType

NCORES = 8
NB = 2
D = 1024
SEQ = 2048
CTX = 256
NLAT = NB * SEQ
NCOL = NLAT + NB * CTX
FH = 2816
EPS = 1e-6
LAMBDA_INIT0 = 0.2
GRID_W = 64

DBG = {"stop": None, "ncores": NCORES}


class Tok:
    __slots__ = ("w", "r")

    def __init__(self):
        self.w = None
        self.r = {}


class Eng:
    def __init__(self, name, e, inorder_safe=False):
        self.name, self.e = name, e
        self.key = "s_" + name
        self.count = 0
        self.seen = {}
        self.inorder_safe = inorder_safe


class Prog:
    def __init__(self, nc, stack, n_dma_sems=32):
        self.nc = nc
        self.sems = {}

        def mk(name):
            self.sems[name] = stack.enter_context(nc.semaphore(name))

        self.pe = Eng("pe", nc.tensor, True)
        self.act = Eng("act", nc.scalar)
        self.dve = Eng("dve", nc.vector)
        self.pool = Eng("pool", nc.gpsimd)
        self.sp = Eng("sp", nc.sync)
        self.engs = [self.pe, self.act, self.dve, self.pool, self.sp]
        for e in self.engs:
            mk(e.key)
        self.dma_pools = {}
        for e, n in ((self.sp, 24), (self.pool, 8), (self.act, 8)):
            lst = []
            for i in range(n):
                mk(f"s_dma_{e.name}{i}")
                lst.append([f"s_dma_{e.name}{i}", 0])
            self.dma_pools[e.name] = [lst, 0]
        self.n_inst = 0

    def _need(self, eng, waits, ev):
        if ev is None:
            return
        k, v = ev
        if eng.seen.get(k, 0) >= v:
            return
        if waits.get(k, 0) < v:
            waits[k] = v

    def _deps(self, eng, reads, writes, extra=None):
        waits = {}
        if extra is not None:
            self._need(eng, waits, extra)
        for t in reads:
            self._need(eng, waits, t.w)
        for t in writes:
            self._need(eng, waits, t.w)
            for k, v in t.r.items():
                self._need(eng, waits, (k, v))
        if eng.inorder_safe and eng.key in waits:
            del waits[eng.key]
        for k, v in waits.items():
            eng.e.wait_ge(self.sems[k], v)
            eng.seen[k] = v

    def _mark(self, ev, reads, writes):
        k, v = ev
        for t in reads:
            if t.r.get(k, 0) < v:
                t.r[k] = v
        for t in writes:
            t.w = ev
            t.r = {}

    def op(self, eng, fn, reads=(), writes=()):
        self._deps(eng, reads, writes)
        inst = fn(eng.e)
        eng.count += 1
        inst.then_inc(self.sems[eng.key], 1)
        self._mark((eng.key, eng.count), reads, writes)
        self.n_inst += 1

    def dma(self, out, in_, reads=(), writes=(), eng=None, **kw):
        eng = eng or self.sp
        pl = self.dma_pools[eng.name]
        slot = pl[0][pl[1] % len(pl[0])]
        pl[1] += 1
        k = slot[0]
        prev = (k, slot[1]) if slot[1] > 0 else None
        self._deps(eng, reads, writes, extra=prev)
        inst = eng.e.dma_start(out=out, in_=in_, **kw)
        slot[1] += 16
        inst.then_inc(self.sems[k], 16)
        self._mark((k, slot[1]), reads, writes)
        self.n_inst += 1

    def barrier(self):
        evs = [(e.key, e.count) for e in self.engs if e.count > 0]
        for pl in self.dma_pools.values():
            evs += [(s[0], s[1]) for s in pl[0] if s[1] > 0]
        for e in self.engs:
            for k, v in evs:
                if k == e.key:
                    continue
                if e.seen.get(k, 0) < v:
                    e.e.wait_ge(self.sems[k], v)
                    e.seen[k] = v

    def mm(self, out, lhsT, rhs, start, stop, reads, writes):
        self.op(self.pe, lambda e: e.matmul(out, lhsT=lhsT, rhs=rhs, start=start, stop=stop),
                reads, writes)

    def tr(self, out, in_, ident, reads, writes):
        self.op(self.pe, lambda e: e.transpose(out, in_, ident), reads, writes)

    def actf(self, out, in_, func, reads, writes, scale=1.0, bias=0.0):
        self.op(self.act, lambda e: e.activation(out=out, in_=in_, func=func, bias=bias, scale=scale),
                reads, writes)

    def tt(self, eng, out, in0, in1, op, reads, writes):
        self.op(eng, lambda e: e.tensor_tensor(out=out, in0=in0, in1=in1, op=op), reads, writes)

    def ts(self, eng, out, in0, s1, op0, reads, writes, s2=None, op1=None):
        if op1 is None:
            self.op(eng, lambda e: e.tensor_scalar(out=out, in0=in0, scalar1=s1, scalar2=None, op0=op0),
                    reads, writes)
        else:
            self.op(eng, lambda e: e.tensor_scalar(out=out, in0=in0, scalar1=s1, scalar2=s2, op0=op0, op1=op1),
                    reads, writes)

    def stt(self, out, in0, scalar, in1, op0, op1, reads, writes):
        self.op(self.dve, lambda e: e.scalar_tensor_tensor(out=out, in0=in0, scalar=scalar, in1=in1,
                                                           op0=op0, op1=op1), reads, writes)

    def copy(self, eng, out, in_, reads, writes):
        if eng is self.act:
            self.op(eng, lambda e: e.copy(out=out, in_=in_), reads, writes)
        else:
            self.op(eng, lambda e: e.tensor_copy(out=out, in_=in_), reads, writes)

    def memset(self, eng, ap, val, writes):
        self.op(eng, lambda e: e.memset(ap, val), (), writes)


def fm(v, n):
    return np.ascontiguousarray(np.asarray(v, np.float32).reshape(n, 128).T)


def wk(w):
    K = w.shape[0] // 128
    return np.ascontiguousarray(w.reshape(K, 128, w.shape[1]).transpose(1, 0, 2))


def rope_perm64():
    i = np.arange(64)
    half = (i % 32) // 16
    return np.where(half == 0, i + 16, i - 16)


def rope_tables():
    t = np.arange(SEQ)
    row = (t // GRID_W).astype(np.float32)
    col = (t % GRID_W).astype(np.float32)
    nf = 16
    freqs = (np.float32(10000.0) ** (-np.arange(nf, dtype=np.float32) / np.float32(nf))).astype(np.float32)
    i = np.arange(64)
    a = i // 32
    half = (i % 32) // 16
    f = i % 16
    pos = np.where(a[:, None] == 0, row[None, :], col[None, :]).astype(np.float32)
    ang = (pos * freqs[f][:, None]).astype(np.float32)
    cos = np.cos(ang).astype(np.float32)
    sin = np.sin(ang).astype(np.float32)
    sgn = np.where(half == 0, -1.0, 1.0).astype(np.float32)[:, None]
    sins = (sin * sgn).astype(np.float32)
    return np.concatenate([cos, cos], 0), np.concatenate([sins, sins], 0)


def prep_shared(inp):
    sh = {}
    f32 = np.float32
    win = np.empty((4, 22, 128, 8, 256), f32)
    wout = np.empty((4, 8, 128, 22, 128), f32)
    for l in range(2):
        for wi, (a, b) in enumerate((("w_ffn1_in", "w_ffn1_out"), ("w_ffn2_in", "w_ffn2_out"))):
            w = np.asarray(inp[a][l], f32)
            wkk = w.reshape(8, 128, 5632).transpose(1, 0, 2)
            g = wkk[:, :, :FH].reshape(128, 8, 22, 128)
            u = wkk[:, :, FH:].reshape(128, 8, 22, 128)
            fi = l * 2 + wi
            win[fi, :, :, :, :128] = g.transpose(2, 0, 1, 3)
            win[fi, :, :, :, 128:] = u.transpose(2, 0, 1, 3)
            wo = np.asarray(inp[b][l], f32)
            wout[fi] = wo.reshape(22, 128, 8, 128).transpose(2, 1, 0, 3)
    sh["win"] = win
    sh["wout"] = wout
    wm = np.asarray(inp["w_mod"], f32).reshape(2, 8, 128, 36, 2, 128)
    sh["wmod"] = np.ascontiguousarray(wm.transpose(0, 3, 2, 4, 1, 5))
    sh["bmod"] = np.ascontiguousarray(np.asarray(inp["b_mod"], f32).reshape(2, 72, 128).transpose(2, 0, 1))
    gs = [inp["g_ffn1"][0], inp["g_mix"][0], inp["g_ffn2"][0], inp["g_ffn1"][1], inp["g_mix"][1],
          inp["g_ffn2"][1], inp["g_final"]]
    sh["gains"] = np.ascontiguousarray(np.stack([fm(g, 8) for g in gs], 1))
    sh["ident"] = np.eye(128, dtype=f32)
    sh["ones"] = np.ones((128, 128), f32)
    w = np.asarray(inp["ab_w_in"][0], f32)
    p64 = rope_perm64()
    p512 = (np.arange(512) // 64) * 64
    p512 = p512 + p64[np.arange(512) % 64]
    qd, kd, vd = w[:, 0:512], w[:, 512:1024], w[:, 1024:1536]
    cq, ckv, kr = w[:, 1536:1792], w[:, 1792:1920], w[:, 1920:1984]
    cols = np.concatenate([qd, qd[:, p512], kd, kd[:, p512], vd, cq, ckv,
                           kr, kr, kr[:, p64], kr[:, p64]], 1)
    assert cols.shape[1] == 3200
    sh["wab"] = wk(cols)
    wuq = np.asarray(inp["ab_w_uq"][0], f32)
    z64 = np.zeros((256, 64), f32)
    qn = [wuq[:, h * 192:h * 192 + 128] for h in range(4)]
    qr = [np.concatenate([wuq[:, h * 192 + 128:h * 192 + 192], z64], 1) for h in range(4)]
    qrp = [np.concatenate([wuq[:, h * 192 + 128:h * 192 + 192][:, p64], z64], 1) for h in range(4)]
    sh["wuq"] = wk(np.concatenate(qn + qr + qrp, 1))
    wukv = np.asarray(inp["ab_w_ukv"][0], f32)
    kn = [wukv[:, h * 256:h * 256 + 128] for h in range(4)]
    vv = [wukv[:, h * 256 + 128:h * 256 + 256] for h in range(4)]
    sh["wukv"] = np.ascontiguousarray(np.concatenate(kn + vv, 1))
    sh["wabo"] = wk(np.asarray(inp["ab_w_out"][0], f32))
    sh["abvec"] = np.ascontiguousarray(np.concatenate(
        [fm(inp["ab_g_q"][0], 2), fm(inp["ab_g_kv"][0], 1), fm(inp["ab_g_subln"][0], 1)], 1))
    sh["lamv"] = np.ascontiguousarray(np.stack(
        [inp["ab_lam_q1"][0], inp["ab_lam_k1"][0], inp["ab_lam_q2"][0], inp["ab_lam_k2"][0]], 0).astype(f32))
    cos, sins = rope_tables()
    sh["cos"] = cos
    sh["sins"] = sins
    w = np.asarray(inp["cd_w_in"][0], f32)
    zc, xbc, dtc = w[:, 0:1024], w[:, 1024:3072], w[:, 3072:3104]
    qc, kc, vc = w[:, 3104:3616], w[:, 3616:4128], w[:, 4128:4640]
    cols = np.concatenate([zc, xbc, qc, kc, vc, dtc, np.zeros((1024, 96), f32)], 1)
    sh["wcd"] = wk(cols)
    cw = np.asarray(inp["cd_conv_w"][0], f32)
    sh["convw"] = np.ascontiguousarray(cw.reshape(5, 16, 128).transpose(2, 1, 0))
    sh["convb"] = fm(inp["cd_conv_b"][0], 16)
    sh["dtb"] = np.ascontiguousarray(np.asarray(inp["cd_dt_bias"][0], f32).reshape(1, 32))
    sh["alog"] = np.ascontiguousarray(np.asarray(inp["cd_a_log"][0], f32).reshape(1, 32))
    dsk = np.asarray(inp["cd_d_skip"][0], f32)
    hidx = (np.arange(1024) // 64)
    sh["dsk"] = np.ascontiguousarray(np.stack([fm(dsk[0][hidx], 8), fm(dsk[1][hidx], 8)], 1))
    sh["gnorm"] = fm(inp["cd_g_norm"][0], 8)
    sh["wcdo"] = wk(np.asarray(inp["cd_w_out"][0], f32))
    sh["nab"] = na_tables(np.asarray(inp["cd_rpb"][0], f32))
    u = np.arange(128)
    U = (u[:, None] <= u[None, :]).astype(f32)
    L = (u[:, None] >= u[None, :]).astype(f32)
    Us = (u[:, None] < u[None, :]).astype(f32)
    Ls = (u[:, None] > u[None, :]).astype(f32)
    sh["masks"] = np.ascontiguousarray(np.stack([U, L, Us, Ls], 1))
    return sh


def na_tables(rpb):
    NEG = np.float32(-30000.0)
    tab = np.full((8, 128, 25, 128), NEG, np.float32)
    ki = np.arange(128)
    qi = np.arange(128)
    for cls, qb in enumerate((0, 1, 7, 14, 15)):
        start = min(max(qb - 2, 0), 11)
        for i in range(5):
            m = start + i
            kr = 2 * m + ki // 64
            kc = ki % 64
            r = 2 * qb + qi // 64
            c = qi % 64
            r0 = np.clip(r - 4, 0, 32 - 8)
            c0 = np.clip(c - 8, 0, 64 - 16)
            valid = ((kr[:, None] >= r0[None, :]) & (kr[:, None] < r0[None, :] + 8) &
                     (kc[:, None] >= c0[None, :]) & (kc[:, None] < c0[None, :] + 16))
            dr_ = np.clip(kr[:, None] - r[None, :] + 7, 0, 14)
            dc_ = np.clip(kc[:, None] - c[None, :] + 15, 0, 30)
            for h in range(8):
                tab[h, :, cls * 5 + i, :] = np.where(valid, rpb[h][dr_, dc_], NEG)
    return tab


def prep_core(inp, core):
    pc = {}
    b0 = core * NB
    rows = np.concatenate([np.asarray(inp["x"][b0 + b], np.float32) for b in range(NB)] +
                          [np.asarray(inp["ctx"][b0 + b], np.float32) for b in range(NB)], 0)
    pc["xT"] = np.ascontiguousarray(rows.reshape(NCOL, 8, 128).transpose(2, 1, 0))
    cc = np.zeros((4, D), np.float32)
    cc[0] = inp["c"][b0]
    cc[1] = inp["c"][b0 + 1]
    cc[2] = inp["c_ctx"]
    pc["cT"] = np.ascontiguousarray(cc.reshape(4, 8, 128).transpose(2, 1, 0))
    return pc


class K:
    pass


def build(shapes):
    nc = bass.Bass("TRN2", target_bir_lowering=False)
    k = K()
    k.nc = nc
    dr = {}
    for name, shp in shapes.items():
        dr[name] = nc.dram_tensor(name, list(shp), F32, kind="ExternalInput").ap()
    k.dr = dr
    k.oT = nc.dram_tensor("oT", [128, 8, NLAT], F32, kind="ExternalOutput").ap()
    if DBG["stop"] is not None:
        k.XT = nc.dram_tensor("XTd", [128, 8, NCOL], F32, kind="ExternalOutput").ap()
    else:
        k.XT = nc.dram_tensor("XTs", [128, 8, NCOL], F32, kind="Internal").ap()

    def scratch(name, shp, dt):
        return nc.dram_tensor(name, list(shp), dt, kind="Internal").ap()

    k.QD = scratch("QD", [128, 4, NCOL], BF16)
    k.KD = scratch("KD", [128, 4, NCOL], BF16)
    k.VD = scratch("VD", [NCOL, 512], BF16)
    k.QN = scratch("QN", [128, 4, NCOL], BF16)
    k.QR = scratch("QR", [128, 4, NCOL], BF16)
    k.KN = scratch("KN", [128, 4, NCOL], BF16)
    k.KR = scratch("KR", [128, NCOL], BF16)
    k.VM = scratch("VM", [NCOL, 512], BF16)
    k.CAT = scratch("CAT", [128, 12, NCOL], BF16)
    k.WBI = scratch("WBI", [4, 22, 128, 8, 256], BF16)
    k.WBO = scratch("WBO", [4, 8, 128, 22, 128], BF16)
    k.tWBI = [[Tok() for _ in range(22)] for _ in range(4)]
    k.tWBO = [[Tok() for _ in range(8)] for _ in range(4)]
    k.SZ = scratch("SZ", [128, 8, NLAT], F32)
    k.XBC = scratch("XBC", [128, 16, NCOL], F32)
    k.XS = scratch("XS", [128, 8, NCOL], F32)
    k.BCT = scratch("BCT", [128, 8, NCOL], BF16)
    k.BTOK = scratch("BTOK", [NCOL, 512], BF16)
    k.NQ = scratch("NQ", [128, 4, NCOL], BF16)
    k.NK = scratch("NK", [128, 4, NCOL], BF16)
    k.NV = scratch("NV", [NCOL, 512], BF16)

    with ExitStack() as st:
        P = Prog(nc, st)
        k.P = P
        k.st = st

        k.uid = 0

        def sb(name, shp, dt, stack=st):
            k.uid += 1
            return stack.enter_context(nc.sbuf_tensor(f"sb{k.uid}_{name}", list(shp), dt))

        k.sb = sb
        k.ps = [st.enter_context(nc.psum_tensor(f"ps{i}", [128, 512], F32)) for i in range(8)]
        k.tps = [Tok() for _ in range(8)]
        k.ident = sb("ident", [128, 128], F32)
        k.ones = sb("ones", [128, 128], F32)
        k.onesb = sb("onesb", [128, 128], BF16)
        k.tconst = Tok()
        P.dma(k.ident[:], dr["ident"], writes=[k.tconst])
        P.dma(k.ones[:], dr["ones"], writes=[k.tconst])
        P.copy(P.dve, k.onesb[:], k.ones[:], [k.tconst], [k.tconst])
        k.mods = sb("mods", [128, 2, 9, 8, 4], F32)
        k.tmods = Tok()
        k.gains = sb("gains", [128, 7, 8], F32)
        P.dma(k.gains[:], dr["gains"], writes=[k.tconst])
        k.nsq = [sb(f"nsq{i}", [128, 512], BF16) for i in range(2)]
        k.tnsq = [Tok() for _ in range(2)]
        k.nln = sb("nln", [128, 512], F32)
        k.tnln = Tok()
        k.nrs = sb("nrs", [128, 512], F32)
        k.tnrs = Tok()
        k.ntmp = [sb(f"ntmp{i}", [128, 512], F32) for i in range(2)]
        k.tntmp = [Tok() for _ in range(2)]
        k.rr = 0

        precast(k, [0])
        precast_small(k, ["wab", "wuq", "wukv"])
        phase_mod(k)
        P.barrier()
        if DBG["stop"] == "mod":
            return finish(k)
        phase_ffn(k, 0, 0, True, src=dr["xT"])
        P.barrier()
        if DBG["stop"] == "ffn1_0":
            return finish(k)
        phase_mix_ab(k)
        P.barrier()
        precast_small(k, ["wabo"])
        precast(k, [1])
        precast_small(k, ["wcd", "wcdo"])
        precast(k, [2, 3])
        phase_att_ab(k)
        P.barrier()
        if DBG["stop"] == "mix0":
            phase_outproj(k, 0, "wabo", 8, True)
            return finish(k)
        phase_outproj(k, 0, "wabo", 8, True)
        phase_ffn(k, 0, 1, True)
        P.barrier()
        if DBG["stop"] == "l0":
            return finish(k)
        phase_ffn(k, 1, 0, True)
        P.barrier()
        if DBG["stop"] == "ffn1_1":
            return finish(k)
        k.DTs = sb("DTs", [128, 36, 32], F32)
        k.ADTs = sb("ADTs", [128, 36, 32], F32)
        k.tdt = Tok()
        phase_mix_cd(k)
        phase_na(k)
        phase_ssd(k)
        phase_outproj(k, 1, "wcdo", 12, False)
        if DBG["stop"] == "mix1":
            return finish(k)
        phase_ffn(k, 1, 1, False, final=True)
        P.barrier()
        return finish(k)


def finish(k):
    k.P.barrier()
    return k.nc


def normT(k, xin, nfeat, W, outs, A=None, B=None, rd=(), wr=()):
    P = k.P
    n = len(xin)
    pb = 7
    for c in range(n):
        i = k.rr % 2
        k.rr += 1
        P.actf(k.nsq[i][:, :W], xin[c], AF.Square, list(rd), [k.tnsq[i]])
        P.mm(k.ps[pb][:, :W], k.onesb[:], k.nsq[i][:, :W], c == 0, c == n - 1,
             [k.tnsq[i], k.tconst], [k.tps[pb]])
    P.actf(k.nln[:, :W], k.ps[pb][:, :W], AF.Ln, [k.tps[pb]], [k.tnln], scale=1.0 / nfeat, bias=EPS)
    P.actf(k.nrs[:, :W], k.nln[:, :W], AF.Exp, [k.tnln], [k.tnrs], scale=-0.5)
    for c in range(n):
        if B is None:
            if A is None:
                P.tt(P.dve, outs[c], xin[c], k.nrs[:, :W], ALU.mult, list(rd) + [k.tnrs], list(wr))
            else:
                P.stt(outs[c], xin[c], A[c], k.nrs[:, :W], ALU.mult, ALU.mult,
                      list(rd) + [k.tnrs, k.tmods], list(wr))
        else:
            i = k.rr % 2
            k.rr += 1
            P.stt(k.ntmp[i][:, :W], xin[c], A[c], k.nrs[:, :W], ALU.mult, ALU.mult,
                  list(rd) + [k.tnrs, k.tmods], [k.tntmp[i]])
            P.actf(outs[c], k.ntmp[i][:, :W], AF.Identity, [k.tntmp[i], k.tmods], list(wr), bias=B[c])


def phase_mod(k):
    P, dr, nc = k.P, k.dr, k.nc
    with ExitStack() as st:
        sb = lambda n, s, d: k.sb(n, s, d, st)
        cT = sb("cT", [128, 8, 4], F32)
        sc = sb("scT", [128, 8, 4], F32)
        bm = sb("bmodT", [128, 2, 72], F32)
        tc_, tb = Tok(), Tok()
        P.dma(cT[:], dr["cT"], writes=[tc_])
        P.dma(bm[:], dr["bmod"], writes=[tb])
        P.actf(sc[:], cT[:], AF.Silu, [tc_], [tc_])
        wst = [sb(f"wmst{i}", [128, 2, 8, 128], F32) for i in range(3)]
        tw = [Tok() for _ in range(3)]
        for l in range(2):
            pb = l
            for g in range(36):
                i = g % 3
                P.dma(wst[i][:], dr["wmod"][l, g], writes=[tw[i]])
                for j2 in range(2):
                    jc = g * 2 + j2
                    for kk in range(8):
                        P.mm(k.ps[pb][:, jc * 4:jc * 4 + 4], wst[i][:, j2, kk, :], sc[:, kk, :],
                             kk == 0, kk == 7, [tw[i], tc_], [k.tps[pb]])
            P.tt(P.dve, k.mods[:, l].rearrange("p j c r -> p (j c) r"),
                 k.ps[pb][:, 0:288].rearrange("p (a r) -> p a r", r=4),
                 bm[:, l, :].unsqueeze(2).to_broadcast([128, 72, 4]), ALU.add,
                 [k.tps[pb], tb], [k.tmods])
        for l in range(2):
            for si, j in enumerate((1, 4, 7)):
                gv = k.gains[:, l * 3 + si, :].unsqueeze(2).to_broadcast([128, 8, 4])
                P.ts(P.dve, k.mods[:, l, j], k.mods[:, l, j], 1.0, ALU.add, [k.tmods], [k.tmods])
                P.tt(P.dve, k.mods[:, l, j], k.mods[:, l, j], gv, ALU.mult, [k.tmods, k.tconst], [k.tmods])
            for j in (2, 8):
                P.ts(P.dve, k.mods[:, l, j], k.mods[:, l, j], 0.5, ALU.mult, [k.tmods], [k.tmods])
        P.barrier()


def precast(k, fis):
    P, dr = k.P, k.dr
    for fi in fis:
        for fp in range(22):
            P.dma(k.WBI[fi, fp], dr["win"][fi, fp], writes=[k.tWBI[fi][fp]], eng=P.pool, max_dma_last_dim=4096)
        for dc in range(8):
            P.dma(k.WBO[fi, dc], dr["wout"][fi, dc], writes=[k.tWBO[fi][dc]], eng=P.pool, max_dma_last_dim=4096)


def precast_small(k, names):
    P, dr, nc = k.P, k.dr, k.nc
    if not hasattr(k, "wbf"):
        k.wbf = {}
        k.twbf = {}
    for name in names:
        src = dr[name]
        shp = list(src.shape)
        dst = nc.dram_tensor("bf_" + name, shp, BF16, kind="Internal").ap()
        k.wbf[name] = dst
        k.twbf[name] = Tok()
        if len(shp) == 3:
            toks = []
            for kk in range(shp[1]):
                t = Tok()
                P.dma(dst[:, kk, :], src[:, kk, :], writes=[t], eng=P.pool, max_dma_last_dim=4096)
                toks.append(t)
            k.twbf[name] = toks
        else:
            t = Tok()
            P.dma(dst, src, writes=[t], eng=P.pool, max_dma_last_dim=4096)
            k.twbf[name] = [t]


def norm_a(k, xin, W, pb, rd):
    P = k.P
    n = len(xin)
    for c in range(n):
        i = k.rr % 2
        k.rr += 1
        P.actf(k.nsq[i][:, :W], xin[c], AF.Square, list(rd), [k.tnsq[i]])
        P.mm(k.ps[pb][:, :W], k.onesb[:], k.nsq[i][:, :W], c == 0, c == n - 1,
             [k.tnsq[i], k.tconst], [k.tps[pb]])


def norm_b(k, xin, nfeat, W, pb, outs, A, B, rs, trs, rd, wr):
    P = k.P
    n = len(xin)
    P.actf(k.nln[:, :W], k.ps[pb][:, :W], AF.Ln, [k.tps[pb]], [k.tnln], scale=1.0 / nfeat, bias=EPS)
    P.actf(rs[:, :W], k.nln[:, :W], AF.Exp, [k.tnln], [trs], scale=-0.5)
    for c in range(n):
        i = k.rr % 2
        k.rr += 1
        P.stt(k.ntmp[i][:, :W], xin[c], A[c], rs[:, :W], ALU.mult, ALU.mult,
              list(rd) + [trs, k.tmods], [k.tntmp[i]])
        P.actf(outs[c], k.ntmp[i][:, :W], AF.Identity, [k.tntmp[i], k.tmods], list(wr), bias=B[c])


def phase_ffn(k, l, wi, do_ctx, final=False, src=None):
    P, dr, nc = k.P, k.dr, k.nc
    fi = l * 2 + wi
    js, jg = (0, 2) if wi == 0 else (6, 8)
    tiles = [(b * SEQ + h * 1024, 1024, b) for b in range(NB) for h in range(2)]
    if do_ctx:
        tiles.append((NLAT, 512, 2))
    xsrc = src if src is not None else k.XT
    with ExitStack() as st:
        sb = lambda n, s, d: k.sb(n, s, d, st)
        xs = [sb(f"f_x{i}", [128, 8, 1024], F32) for i in range(2)]
        hT = [sb(f"f_h{i}", [128, 8, 1024], BF16) for i in range(2)]
        hid = sb("f_hid", [128, 22, 1024], BF16)
        wbi = [sb(f"f_wbi{i}", [128, 8, 256], BF16) for i in range(3)]
        wbo = [sb(f"f_wbo{i}", [128, 22, 128], BF16) for i in range(2)]
        sg = [sb(f"f_sg{i}", [128, 512], F32) for i in range(2)]
        rs2 = [sb(f"f_rs{i}", [128, 512], F32) for i in range(2)]
        tx, th = [Tok(), Tok()], [Tok(), Tok()]
        thid = Tok()
        twbi, twbo = [Tok(), Tok(), Tok()], [Tok(), Tok()]
        tsg, trs2 = [Tok(), Tok()], [Tok(), Tok()]
        cnt = 0
        nt = len(tiles)

        def load_x(ti):
            c0, W, r = tiles[ti]
            P.dma(xs[ti % 2][:, :, :W], xsrc[:, :, c0:c0 + W], writes=[tx[ti % 2]])

        def norm_tile(ti):
            c0, W, r = tiles[ti]
            u = ti % 2
            NS = W // 512
            for s in range(NS):
                sl = slice(s * 512, (s + 1) * 512)
                norm_a(k, [xs[u][:, c, sl] for c in range(8)], 512, 6 + s, [tx[u]])
            for s in range(NS):
                sl = slice(s * 512, (s + 1) * 512)
                norm_b(k, [xs[u][:, c, sl] for c in range(8)], D, 512, 6 + s, [hT[u][:, c, sl] for c in range(8)],
                       [k.mods[:, l, js + 1, c, r:r + 1] for c in range(8)],
                       [k.mods[:, l, js, c, r:r + 1] for c in range(8)], rs2[s], trs2[s], [tx[u]], [th[u]])

        load_x(0)
        norm_tile(0)
        for ti, (c0, W, r) in enumerate(tiles):
            u = ti % 2
            NS = W // 512
            if ti + 1 < nt:
                load_x(ti + 1)
            P.dma(wbi[0][:], k.WBI[fi, 0], reads=[k.tWBI[fi][0]], writes=[twbi[0]])
            P.dma(wbi[1][:], k.WBI[fi, 1], reads=[k.tWBI[fi][1]], writes=[twbi[1]])
            for fp in range(22):
                i = fp % 3
                if fp + 2 < 22:
                    P.dma(wbi[(fp + 2) % 3][:], k.WBI[fi, fp + 2], reads=[k.tWBI[fi][fp + 2]], writes=[twbi[(fp + 2) % 3]])
                for s in range(NS):
                    sl = slice(s * 512, (s + 1) * 512)
                    pg, pu = 2 + (cnt % 2), 4 + (cnt % 2)
                    j = cnt % 2
                    cnt += 1
                    for kk in range(8):
                        P.mm(k.ps[pg][:], wbi[i][:, kk, 0:128], hT[u][:, kk, sl], kk == 0, kk == 7,
                             [twbi[i], th[u]], [k.tps[pg]])
                    for kk in range(8):
                        P.mm(k.ps[pu][:], wbi[i][:, kk, 128:256], hT[u][:, kk, sl], kk == 0, kk == 7,
                             [twbi[i], th[u]], [k.tps[pu]])
                    P.actf(sg[j][:], k.ps[pg][:], AF.Silu, [k.tps[pg]], [tsg[j]])
                    P.tt(P.dve, hid[:, fp, sl], sg[j][:], k.ps[pu][:], ALU.mult, [tsg[j], k.tps[pu]], [thid])
            P.dma(wbo[0][:], k.WBO[fi, 0], reads=[k.tWBO[fi][0]], writes=[twbo[0]])
            if ti + 1 < nt:
                norm_tile(ti + 1)
            for dc in range(8):
                i = dc % 2
                if dc + 1 < 8:
                    P.dma(wbo[1 - i][:], k.WBO[fi, dc + 1], reads=[k.tWBO[fi][dc + 1]], writes=[twbo[1 - i]])
                for s in range(NS):
                    sl = slice(s * 512, (s + 1) * 512)
                    pb = cnt % 2
                    cnt += 1
                    for kf in range(22):
                        P.mm(k.ps[pb][:], wbo[i][:, kf, :], hid[:, kf, sl], kf == 0, kf == 21,
                             [twbo[i], thid], [k.tps[pb]])
                    xa = xs[u][:, dc, sl]
                    P.stt(xa, k.ps[pb][:], k.mods[:, l, jg, dc, r:r + 1], xa, ALU.mult, ALU.add,
                          [k.tps[pb], tx[u], k.tmods], [tx[u]])
            if not final:
                P.dma(k.XT[:, :, c0:c0 + W], xs[u][:, :, :W], reads=[tx[u]], writes=[Tok()])
            else:
                for s in range(NS):
                    sl = slice(s * 512, (s + 1) * 512)
                    normT(k, [xs[u][:, c, sl] for c in range(8)], D, 512, [xs[u][:, c, sl] for c in range(8)],
                          A=[k.gains[:, 6, c:c + 1] for c in range(8)], rd=[tx[u]], wr=[tx[u]])
                P.dma(k.oT[:, :, c0:c0 + W], xs[u][:, :, :W], reads=[tx[u]], writes=[Tok()])
        P.barrier()


def phase_mix_ab(k):
    P, dr, nc = k.P, k.dr, k.nc
    l = 0
    tiles = [(b * SEQ + h * 512, b, h * 512) for b in range(NB) for h in range(4)] + [(NLAT, 2, None)]
    with ExitStack() as st:
        sb = lambda n, s, d: k.sb(n, s, d, st)
        wabb = sb("m_wab", [128, 8, 3200], BF16)
        twab = Tok()
        for kk in range(8):
            P.dma(wabb[:, kk, :], k.wbf["wab"][:, kk, :], reads=[k.twbf["wab"][kk]], writes=[twab])
        wuqb = sb("m_wuqb", [128, 2, 1536], BF16)
        wukvb = sb("m_wukvb", [128, 1024], BF16)
        tw2 = Tok()
        P.dma(wuqb[:], k.wbf["wuq"], reads=k.twbf["wuq"], writes=[tw2])
        P.dma(wukvb[:], k.wbf["wukv"], reads=k.twbf["wukv"], writes=[tw2])
        cos = sb("m_cos", [128, 2048], F32)
        sins = sb("m_sins", [128, 2048], F32)
        abv = sb("m_abv", [128, 4], F32)
        tcs = Tok()
        P.dma(cos[:], dr["cos"], writes=[tcs])
        P.dma(sins[:], dr["sins"], writes=[tcs])
        P.dma(abv[:], dr["abvec"], writes=[tcs])
        xs = sb("m_x", [128, 8, 512], F32)
        hT = sb("m_h", [128, 8, 512], BF16)
        tx, th = Tok(), Tok()
        ra = [sb(f"m_ra{i}", [128, 512], F32) for i in range(2)]
        rb = [sb(f"m_rb{i}", [128, 512], F32) for i in range(2)]
        tra, trb = [Tok(), Tok()], [Tok(), Tok()]
        oq = sb("m_oq", [128, 4, 512], BF16)
        okd = sb("m_okd", [128, 4, 512], BF16)
        ovd = sb("m_ovd", [128, 4, 512], BF16)
        oqn = sb("m_oqn", [128, 4, 512], BF16)
        oqr = sb("m_oqr", [128, 4, 512], BF16)
        okn = sb("m_okn", [128, 4, 512], BF16)
        okr = sb("m_okr", [128, 512], BF16)
        ovm = sb("m_ovm", [128, 4, 512], BF16)
        toq, tokd, tovd, toqn, toqr, tokn, tokr, tovm = [Tok() for _ in range(8)]
        cqn = sb("m_cqn", [128, 2, 512], BF16)
        ckvn = sb("m_ckvn", [128, 512], BF16)
        tcqn, tckvn = Tok(), Tok()
        st_ = {"b": 0, "r": 0, "c": 0}

        def nb():
            st_["b"] = (st_["b"] + 1) % 6
            return st_["b"]

        def evac(out, pb, tok):
            st_["c"] += 1
            P.copy(P.act if st_["c"] % 2 else P.dve, out, k.ps[pb][:], [k.tps[pb]], [tok])

        def rope(p1, p2, out, tok, t0):
            i = st_["r"] % 2
            st_["r"] += 1
            P.tt(P.dve, ra[i][:], k.ps[p1][:], cos[:, t0:t0 + 512], ALU.mult, [k.tps[p1], tcs], [tra[i]])
            P.tt(P.dve, rb[i][:], k.ps[p2][:], sins[:, t0:t0 + 512], ALU.mult, [k.tps[p2], tcs], [trb[i]])
            P.tt(P.pool, out, ra[i][:], rb[i][:], ALU.add, [tra[i], trb[i]], [tok])

        def proj8(pb, off):
            for kk in range(8):
                P.mm(k.ps[pb][:], wabb[:, kk, off:off + 128], hT[:, kk, :], kk == 0, kk == 7,
                     [twab, th], [k.tps[pb]])

        for (c0, r, t0) in tiles:
            latent = r != 2
            P.dma(xs[:], k.XT[:, :, c0:c0 + 512], writes=[tx])
            normT(k, [xs[:, c, :] for c in range(8)], D, 512, [hT[:, c, :] for c in range(8)],
                  A=[k.mods[:, l, 4, c, r:r + 1] for c in range(8)],
                  B=[k.mods[:, l, 3, c, r:r + 1] for c in range(8)], rd=[tx], wr=[th])
            for (off, offp, dst, ost, tost) in ((0, 512, k.QD, oq, toq), (1024, 1536, k.KD, okd, tokd)):
                for h in range(4):
                    p1 = nb()
                    proj8(p1, off + h * 128)
                    if latent:
                        p2 = nb()
                        proj8(p2, offp + h * 128)
                        rope(p1, p2, ost[:, h, :], tost, t0)
                    else:
                        evac(ost[:, h, :], p1, tost)
                P.dma(dst[:, :, c0:c0 + 512], ost[:], reads=[tost], writes=[Tok()], eng=P.pool)
            for s in range(4):
                p1 = nb()
                for kk in range(8):
                    P.mm(k.ps[p1][:], hT[:, kk, s * 128:(s + 1) * 128], wabb[:, kk, 2048:2560], kk == 0, kk == 7,
                         [twab, th], [k.tps[p1]])
                evac(ovd[:, s, :], p1, tovd)
            P.dma(k.VD[c0:c0 + 512, :].rearrange("(s p) e -> p s e", p=128), ovd[:], reads=[tovd], writes=[Tok()], eng=P.pool)
            pa, pb_ = nb(), nb()
            proj8(pa, 2560)
            proj8(pb_, 2688)
            normT(k, [k.ps[pa][:], k.ps[pb_][:]], 256, 512, [cqn[:, 0, :], cqn[:, 1, :]],
                  A=[abv[:, 0:1], abv[:, 1:2]], rd=[k.tps[pa], k.tps[pb_], tcs], wr=[tcqn])
            pc_ = nb()
            proj8(pc_, 2816)
            normT(k, [k.ps[pc_][:]], 128, 512, [ckvn[:]], A=[abv[:, 2:3]], rd=[k.tps[pc_], tcs], wr=[tckvn])
            p1 = nb()
            proj8(p1, 2944)
            if latent:
                p2 = nb()
                proj8(p2, 3072)
                rope(p1, p2, okr[:], tokr, t0)
            else:
                evac(okr[:], p1, tokr)
            P.dma(k.KR[:, c0:c0 + 512], okr[:], reads=[tokr], writes=[Tok()], eng=P.pool)
            for h in range(4):
                p1 = nb()
                for kk in range(2):
                    P.mm(k.ps[p1][:], wuqb[:, kk, h * 128:(h + 1) * 128], cqn[:, kk, :], kk == 0, kk == 1,
                         [tw2, tcqn], [k.tps[p1]])
                evac(oqn[:, h, :], p1, toqn)
            P.dma(k.QN[:, :, c0:c0 + 512], oqn[:], reads=[toqn], writes=[Tok()], eng=P.pool)
            for h in range(4):
                p1 = nb()
                for kk in range(2):
                    P.mm(k.ps[p1][:], wuqb[:, kk, 512 + h * 128:512 + (h + 1) * 128], cqn[:, kk, :], kk == 0, kk == 1,
                         [tw2, tcqn], [k.tps[p1]])
                if latent:
                    p2 = nb()
                    for kk in range(2):
                        P.mm(k.ps[p2][:], wuqb[:, kk, 1024 + h * 128:1024 + (h + 1) * 128], cqn[:, kk, :],
                             kk == 0, kk == 1, [tw2, tcqn], [k.tps[p2]])
                    rope(p1, p2, oqr[:, h, :], toqr, t0)
                else:
                    evac(oqr[:, h, :], p1, toqr)
            P.dma(k.QR[:, :, c0:c0 + 512], oqr[:], reads=[toqr], writes=[Tok()], eng=P.pool)
            for h in range(4):
                p1 = nb()
                P.mm(k.ps[p1][:], wukvb[:, h * 128:(h + 1) * 128], ckvn[:], True, True, [tw2, tckvn], [k.tps[p1]])
                evac(okn[:, h, :], p1, tokn)
            P.dma(k.KN[:, :, c0:c0 + 512], okn[:], reads=[tokn], writes=[Tok()], eng=P.pool)
            for s in range(4):
                p1 = nb()
                P.mm(k.ps[p1][:], ckvn[:, s * 128:(s + 1) * 128], wukvb[:, 512:1024], True, True,
                     [tw2, tckvn], [k.tps[p1]])
                evac(ovm[:, s, :], p1, tovm)
            P.dma(k.VM[c0:c0 + 512, :].rearrange("(s p) e -> p s e", p=128), ovm[:], reads=[tovm], writes=[Tok()], eng=P.pool)
        P.barrier()


def phase_att_ab(k):
    P, dr, nc = k.P, k.dr, k.nc
    with ExitStack() as st:
        sb = lambda n, s, d: k.sb(n, s, d, st)
        kdh = sb("a_kdh", [128, 2304], BF16)
        vdh = sb("a_vdh", [128, 18, 128], BF16)
        krb = sb("a_krb", [128, 2304], BF16)
        qdt = sb("a_qdt", [128, 512], BF16)
        qrt = sb("a_qrt", [128, 512], BF16)
        q1 = sb("a_q1", [128, 512], BF16)
        q2 = sb("a_q2", [128, 512], BF16)
        pt = [sb(f"a_pt{i}", [128, 512], BF16) for i in range(2)]
        r1 = sb("a_r1", [128, 512], F32)
        r2 = sb("a_r2", [128, 512], F32)
        t1 = sb("a_t1", [128, 512], F32)
        t2 = sb("a_t2", [128, 512], F32)
        oo = sb("a_oo", [128, 512], F32)
        cato = [sb(f"a_cato{i}", [128, 512], BF16) for i in range(2)]
        lv = sb("a_lv", [128, 4, 64], F32)
        ltmp = sb("a_ltmp", [128, 64], F32)
        lsc = sb("a_lsc", [128, 4], F32)
        abv = sb("a_abv", [128, 4], F32)
        gsub = sb("a_gsub", [128, 1], F32)
        tkd, tvd, tkr, tq, tqr, tq1, tq2 = [Tok() for _ in range(7)]
        tpt = [Tok(), Tok()]
        tr1, tr2, tt1, tt2, too = [Tok() for _ in range(5)]
        tcato = [Tok(), Tok()]
        tl = Tok()
        P.dma(lv[:], dr["lamv"].partition_broadcast(128), writes=[tl])
        P.dma(abv[:], dr["abvec"], writes=[tl])
        for i in range(2):
            P.op(P.dve, lambda e, i=i: e.scalar_tensor_tensor(
                out=ltmp[:], in0=lv[:, 2 * i, :], scalar=1.0, in1=lv[:, 2 * i + 1, :],
                op0=ALU.mult, op1=ALU.mult, accum_out=lsc[:, i:i + 1]), [tl], [tl])
        P.actf(lsc[:, 0:2], lsc[:, 0:2], AF.Exp, [tl], [tl])
        P.tt(P.dve, lsc[:, 2:3], lsc[:, 1:2], lsc[:, 0:1], ALU.subtract, [tl], [tl])
        P.ts(P.dve, lsc[:, 3:4], lsc[:, 2:3], -LAMBDA_INIT0, ALU.add, [tl], [tl])
        P.ts(P.dve, gsub[:], abv[:, 3:4], 1.0 - LAMBDA_INIT0, ALU.mult, [tl], [tl])
        neglam = lsc[:, 3:4]
        ptb = pt + [sb("a_pt2", [128, 512], BF16)]
        tptb = tpt + [Tok()]
        SB_ = [0, 1, 6]
        st_ = {"n": 0, "pend": None}
        eO = [sb(f"a_eO{i}", [128, 512], F32) for i in range(2)]
        eD = [sb(f"a_eD{i}", [128, 512], F32) for i in range(2)]
        teO, teD = [Tok(), Tok()], [Tok(), Tok()]

        def run_pipe(A, B, depth=2):
            n = len(A)
            for i in range(n + depth):
                if i < n:
                    A[i]()
                if i - depth >= 0:
                    B[i - depth]()
                if i == 5 and st_["pend"] is not None:
                    st_["pend"]()
                    st_["pend"] = None
            if st_["pend"] is not None and n <= 5:
                st_["pend"]()
                st_["pend"] = None

        ci = 0
        for b in range(NB):
            lat0, cx0 = b * SEQ, NLAT + b * CTX
            qtiles = [(lat0 + q * 512, 512, list(range(18))) for q in range(4)] + [(cx0, 256, [16, 17])]
            for h in range(4):
                P.dma(kdh[:, 0:2048], k.KD[:, h, lat0:lat0 + 2048], writes=[tkd])
                P.dma(kdh[:, 2048:2304], k.KD[:, h, cx0:cx0 + 256], writes=[tkd])
                P.dma(vdh[:, 0:16, :], k.VD[lat0:lat0 + 2048, h * 128:(h + 1) * 128].rearrange("(c p) e -> p c e", p=128),
                      writes=[tvd])
                P.dma(vdh[:, 16:18, :], k.VD[cx0:cx0 + 256, h * 128:(h + 1) * 128].rearrange("(c p) e -> p c e", p=128),
                      writes=[tvd])
                for (q0, W, chunks) in qtiles:
                    P.dma(qdt[:, :W], k.QD[:, h, q0:q0 + W], writes=[tq])
                    P.copy(P.dve, q1[0:64, :W], qdt[0:64, :W], [tq], [tq1])
                    P.memset(P.dve, q1[64:128, :W], 0.0, [tq1])
                    P.copy(P.dve, q2[64:128, :W], qdt[64:128, :W], [tq], [tq2])
                    P.memset(P.dve, q2[0:64, :W], 0.0, [tq2])
                    A, B = [], []
                    nch = len(chunks)
                    for comp in range(2):
                        qm, tqm = (q1, tq1) if comp == 0 else (q2, tq2)
                        for idx, kc in enumerate(chunks):
                            j = st_["n"] % 3
                            st_["n"] += 1

                            def a_(j=j, kc=kc, qm=qm, tqm=tqm, W=W):
                                pb = SB_[j]
                                P.mm(k.ps[pb][:, :W], kdh[:, kc * 128:(kc + 1) * 128], qm[:, :W], True, True,
                                     [tkd, tqm], [k.tps[pb]])
                                P.actf(ptb[j][:, :W], k.ps[pb][:, :W], AF.Exp, [k.tps[pb]], [tptb[j]], scale=0.125)

                            def b_(j=j, kc=kc, comp=comp, idx=idx, W=W, nch=nch):
                                P.mm(k.ps[2 + comp][:, :W], vdh[:, kc, :], ptb[j][:, :W], idx == 0, idx == nch - 1,
                                     [tvd, tptb[j]], [k.tps[2 + comp]])
                                P.mm(k.ps[4 + comp][:, :W], k.onesb[:], ptb[j][:, :W], idx == 0, idx == nch - 1,
                                     [k.tconst, tptb[j]], [k.tps[4 + comp]])
                            A.append(a_)
                            B.append(b_)
                    run_pipe(A, B)
                    P.copy(P.act, eO[0][:, :W], k.ps[2][:, :W], [k.tps[2]], [teO[0]])
                    P.copy(P.dve, eD[0][:, :W], k.ps[4][:, :W], [k.tps[4]], [teD[0]])
                    P.copy(P.act, eO[1][:, :W], k.ps[3][:, :W], [k.tps[3]], [teO[1]])
                    P.copy(P.dve, eD[1][:, :W], k.ps[5][:, :W], [k.tps[5]], [teD[1]])

                    def comb(W=W, h=h, q0=q0):
                        P.op(P.dve, lambda e: e.reciprocal(out=r1[:, :W], in_=eD[0][:, :W]), [teD[0]], [tr1])
                        P.op(P.dve, lambda e: e.reciprocal(out=r2[:, :W], in_=eD[1][:, :W]), [teD[1]], [tr2])
                        P.tt(P.dve, t1[:, :W], eO[0][:, :W], r1[:, :W], ALU.mult, [teO[0], tr1], [tt1])
                        P.tt(P.dve, t2[:, :W], eO[1][:, :W], r2[:, :W], ALU.mult, [teO[1], tr2], [tt2])
                        P.stt(oo[:, :W], t2[:, :W], neglam, t1[:, :W], ALU.mult, ALU.add, [tt1, tt2, tl], [too])
                        st_["c"] = st_.get("c", 0) + 1
                        co = st_["c"] % 2
                        normT(k, [oo[:, :W]], 128, W, [cato[co][:, :W]], A=[gsub[:, 0:1]], rd=[too, tl], wr=[tcato[co]])
                        P.dma(k.CAT[:, h, q0:q0 + W], cato[co][:, :W], reads=[tcato[co]], writes=[Tok()])
                    if st_["pend"] is not None:
                        st_["pend"]()
                    st_["pend"] = comb
            P.dma(krb[:, 0:2048], k.KR[:, lat0:lat0 + 2048], writes=[tkr])
            P.dma(krb[:, 2048:2304], k.KR[:, cx0:cx0 + 256], writes=[tkr])
            msc = float(192.0 ** -0.5)
            for h in range(4):
                P.dma(kdh[:, 0:2048], k.KN[:, h, lat0:lat0 + 2048], writes=[tkd])
                P.dma(kdh[:, 2048:2304], k.KN[:, h, cx0:cx0 + 256], writes=[tkd])
                P.dma(vdh[:, 0:16, :], k.VM[lat0:lat0 + 2048, h * 128:(h + 1) * 128].rearrange("(c p) e -> p c e", p=128),
                      writes=[tvd])
                P.dma(vdh[:, 16:18, :], k.VM[cx0:cx0 + 256, h * 128:(h + 1) * 128].rearrange("(c p) e -> p c e", p=128),
                      writes=[tvd])
                for (q0, W, chunks) in qtiles:
                    P.dma(qdt[:, :W], k.QN[:, h, q0:q0 + W], writes=[tq])
                    P.dma(qrt[:, :W], k.QR[:, h, q0:q0 + W], writes=[tqr])
                    A, B = [], []
                    nch = len(chunks)
                    for idx, kc in enumerate(chunks):
                        j = st_["n"] % 3
                        st_["n"] += 1

                        def a_(j=j, kc=kc, W=W):
                            pb = SB_[j]
                            P.mm(k.ps[pb][:, :W], kdh[:, kc * 128:(kc + 1) * 128], qdt[:, :W], True, False,
                                 [tkd, tq], [k.tps[pb]])
                            P.mm(k.ps[pb][:, :W], krb[:, kc * 128:(kc + 1) * 128], qrt[:, :W], False, True,
                                 [tkr, tqr], [k.tps[pb]])
                            P.actf(ptb[j][:, :W], k.ps[pb][:, :W], AF.Exp, [k.tps[pb]], [tptb[j]], scale=msc)

                        def b_(j=j, kc=kc, idx=idx, W=W, nch=nch):
                            P.mm(k.ps[2][:, :W], vdh[:, kc, :], ptb[j][:, :W], idx == 0, idx == nch - 1,
                                 [tvd, tptb[j]], [k.tps[2]])
                            P.mm(k.ps[4][:, :W], k.onesb[:], ptb[j][:, :W], idx == 0, idx == nch - 1,
                                 [k.tconst, tptb[j]], [k.tps[4]])
                        A.append(a_)
                        B.append(b_)
                    run_pipe(A, B)
                    P.copy(P.act, eO[0][:, :W], k.ps[2][:, :W], [k.tps[2]], [teO[0]])
                    P.copy(P.dve, eD[0][:, :W], k.ps[4][:, :W], [k.tps[4]], [teD[0]])

                    def comb(W=W, h=h, q0=q0):
                        P.op(P.dve, lambda e: e.reciprocal(out=r1[:, :W], in_=eD[0][:, :W]), [teD[0]], [tr1])
                        st_["c"] = st_.get("c", 0) + 1
                        co = st_["c"] % 2
                        P.tt(P.dve, cato[co][:, :W], eO[0][:, :W], r1[:, :W], ALU.mult, [teO[0], tr1], [tcato[co]])
                        P.dma(k.CAT[:, 4 + h, q0:q0 + W], cato[co][:, :W], reads=[tcato[co]], writes=[Tok()])
                    if st_["pend"] is not None:
                        st_["pend"]()
                    st_["pend"] = comb
        if st_["pend"] is not None:
            st_["pend"]()
            st_["pend"] = None
        P.barrier()


def phase_outproj(k, l, wname, nk, do_ctx):
    P, dr, nc = k.P, k.dr, k.nc
    tiles = [(b * SEQ + h * 1024, 1024, b) for b in range(NB) for h in range(2)]
    if do_ctx:
        tiles.append((NLAT, 512, 2))
    with ExitStack() as st:
        sb = lambda n, s, d: k.sb(n, s, d, st)
        xs = sb("o_x", [128, 8, 1024], F32)
        wmo = sb("o_wmo", [128, nk, 1024], BF16)
        cat = sb("o_cat", [128, nk, 1024], BF16)
        tx, twm, tcat = Tok(), Tok(), Tok()
        P.dma(wmo[:], k.wbf[wname], reads=k.twbf[wname], writes=[twm])
        cnt = 0
        for (c0, W, r) in tiles:
            NS = W // 512
            P.dma(xs[:, :, :W], k.XT[:, :, c0:c0 + W], writes=[tx])
            P.dma(cat[:, :, :W], k.CAT[:, 0:nk, c0:c0 + W], writes=[tcat])
            for dc in range(8):
                for s in range(NS):
                    pb = cnt % 4
                    cnt += 1
                    for kk in range(nk):
                        P.mm(k.ps[pb][:], wmo[:, kk, dc * 128:(dc + 1) * 128], cat[:, kk, s * 512:(s + 1) * 512],
                             kk == 0, kk == nk - 1, [twm, tcat], [k.tps[pb]])
                    xa = xs[:, dc, s * 512:(s + 1) * 512]
                    P.stt(xa, k.ps[pb][:], k.mods[:, l, 5, dc, r:r + 1], xa, ALU.mult, ALU.add,
                          [k.tps[pb], tx, k.tmods], [tx])
            P.dma(k.XT[:, :, c0:c0 + W], xs[:, :, :W], reads=[tx], writes=[Tok()])
        P.barrier()


def phase_mix_cd(k):
    P, dr, nc = k.P, k.dr, k.nc
    l = 1
    tiles = [(b * SEQ + h * 512, b) for b in range(NB) for h in range(4)] + [(NLAT, 2)]
    OZ, OX, OQ, OK_, OV, ODT = 0, 1024, 3072, 3584, 4096, 4608
    with ExitStack() as st:
        sb = lambda n, s, d: k.sb(n, s, d, st)
        wb = sb("c_wb", [128, 8, 4736], BF16)
        tw = Tok()
        for kk in range(8):
            P.dma(wb[:, kk, :], k.wbf["wcd"][:, kk, :], reads=[k.twbf["wcd"][kk]], writes=[tw])
        dtb = sb("c_dtb", [128, 1, 32], F32)
        abc = sb("c_abc", [128, 1, 32], F32)
        tcs = Tok()
        P.dma(dtb[:], dr["dtb"].partition_broadcast(128), writes=[tcs])
        P.dma(abc[:], dr["alog"].partition_broadcast(128), writes=[tcs])
        P.actf(abc[:], abc[:], AF.Exp, [tcs], [tcs])
        P.ts(P.dve, abc[:], abc[:], -1.0, ALU.mult, [tcs], [tcs])
        xs = sb("c_x", [128, 8, 512], F32)
        hT = sb("c_h", [128, 8, 512], BF16)
        tx, th = Tok(), Tok()
        osz = sb("c_osz", [128, 8, 512], F32)
        oxbc = sb("c_oxbc", [128, 16, 512], F32)
        onq = sb("c_onq", [128, 4, 512], BF16)
        onk = sb("c_onk", [128, 4, 512], BF16)
        onv = sb("c_onv", [128, 4, 512], BF16)
        tosz, toxbc, tonq, tonk, tonv = [Tok() for _ in range(5)]
        d1 = sb("c_d1", [128, 32], F32)
        d2 = sb("c_d2", [128, 32], F32)
        d3 = sb("c_d3", [128, 32], F32)
        td = Tok()
        st_ = {"b": 0, "c": 0}

        def nb():
            st_["b"] = (st_["b"] + 1) % 6
            return st_["b"]

        def evac(out, pb, tok):
            st_["c"] += 1
            P.copy(P.act if st_["c"] % 2 else P.dve, out, k.ps[pb][:], [k.tps[pb]], [tok])

        def proj8(pb, off):
            for kk in range(8):
                P.mm(k.ps[pb][:], wb[:, kk, off:off + 128], hT[:, kk, :], kk == 0, kk == 7, [tw, th], [k.tps[pb]])

        for (c0, r) in tiles:
            latent = r != 2
            P.dma(xs[:], k.XT[:, :, c0:c0 + 512], writes=[tx])
            normT(k, [xs[:, c, :] for c in range(8)], D, 512, [hT[:, c, :] for c in range(8)],
                  A=[k.mods[:, l, 4, c, r:r + 1] for c in range(8)],
                  B=[k.mods[:, l, 3, c, r:r + 1] for c in range(8)], rd=[tx], wr=[th])
            if latent:
                for j in range(8):
                    p1 = nb()
                    proj8(p1, OZ + j * 128)
                    P.actf(osz[:, j, :], k.ps[p1][:], AF.Silu, [k.tps[p1]], [tosz])
                P.dma(k.SZ[:, :, c0:c0 + 512], osz[:], reads=[tosz], writes=[Tok()], eng=P.pool)
            for j in range(16):
                p1 = nb()
                proj8(p1, OX + j * 128)
                evac(oxbc[:, j, :], p1, toxbc)
            P.dma(k.XBC[:, :, c0:c0 + 512], oxbc[:], reads=[toxbc], writes=[Tok()], eng=P.pool)
            if latent:
                for j in range(4):
                    p1 = nb()
                    proj8(p1, OQ + j * 128)
                    evac(onq[:, j, :], p1, tonq)
                P.dma(k.NQ[:, :, c0:c0 + 512], onq[:], reads=[tonq], writes=[Tok()], eng=P.pool)
            for j in range(4):
                p1 = nb()
                proj8(p1, OK_ + j * 128)
                evac(onk[:, j, :], p1, tonk)
            P.dma(k.NK[:, :, c0:c0 + 512], onk[:], reads=[tonk], writes=[Tok()], eng=P.pool)
            for s in range(4):
                p1 = nb()
                for kk in range(8):
                    P.mm(k.ps[p1][:], hT[:, kk, s * 128:(s + 1) * 128], wb[:, kk, OV:OV + 512], kk == 0, kk == 7,
                         [tw, th], [k.tps[p1]])
                evac(onv[:, s, :], p1, tonv)
            P.dma(k.NV[c0:c0 + 512, :].rearrange("(s p) e -> p s e", p=128), onv[:], reads=[tonv], writes=[Tok()], eng=P.pool)
            for s in range(4):
                q = (c0 + s * 128) // 128
                p1 = nb()
                for kk in range(8):
                    P.mm(k.ps[p1][:, 0:32], hT[:, kk, s * 128:(s + 1) * 128], wb[:, kk, ODT:ODT + 32], kk == 0, kk == 7,
                         [tw, th], [k.tps[p1]])
                P.tt(P.dve, d1[:], k.ps[p1][:, 0:32], dtb[:, 0, :], ALU.add, [k.tps[p1], tcs], [td])
                P.actf(d2[:], d1[:], AF.Abs, [td], [td])
                P.actf(d2[:], d2[:], AF.Exp, [td], [td], scale=-1.0)
                P.actf(d2[:], d2[:], AF.Ln, [td], [td], bias=1.0)
                P.ts(P.dve, d3[:], d1[:], 0.0, ALU.max, [td], [td])
                P.tt(P.dve, k.DTs[:, q, :], d3[:], d2[:], ALU.add, [td], [k.tdt])
                P.tt(P.dve, k.ADTs[:, q, :], k.DTs[:, q, :], abc[:, 0, :], ALU.mult, [k.tdt, tcs], [k.tdt])
        P.barrier()


def conv_setup(k, st):
    P, dr, nc = k.P, k.dr, k.nc
    sb = lambda n, s, d: k.sb(n, s, d, st)
    cw = sb("v_cw", [128, 16, 5], F32)
    cb = sb("v_cb", [128, 16], F32)
    tc_ = Tok()
    P.dma(cw[:], dr["convw"], writes=[tc_])
    P.dma(cb[:], dr["convb"], writes=[tc_])
    buf = [sb(f"v_buf{i}", [128, 2052], F32) for i in range(2)]
    tbuf = [Tok(), Tok()]
    for i in range(2):
        P.memset(P.dve, buf[i][:, 0:2], 0.0, [tbuf[i]])
        P.memset(P.dve, buf[i][:, 2050:2052], 0.0, [tbuf[i]])
    acc = sb("v_acc", [128, 2048], F32)
    so = sb("v_so", [128, 2048], F32)
    sob = sb("v_sob", [128, 2048], BF16)
    obt = sb("v_obt", [128, 16, 128], BF16)
    tacc, tso, tsob, tobt = Tok(), Tok(), Tok(), Tok()
    items = [(c0, W, ch) for b in range(NB) for (c0, W) in ((b * SEQ, SEQ), (NLAT + b * CTX, CTX))
             for ch in range(16)]

    def load(n):
        c0, W, ch = items[n]
        i = n % 2
        if W < SEQ:
            P.memset(P.dve, buf[i][:, 2 + W:4 + W], 0.0, [tbuf[i]])
        P.dma(buf[i][:, 2:2 + W], k.XBC[:, ch, c0:c0 + W], writes=[tbuf[i]])

    def item(n):
        c0, W, ch = items[n]
        i = n % 2
        if n == 0:
            load(0)
        if n + 1 < len(items):
            load(n + 1)
        P.actf(acc[:, :W], buf[i][:, 0:W], AF.Identity, [tbuf[i], tc_], [tacc],
               scale=cw[:, ch, 0:1], bias=cb[:, ch:ch + 1])
        yield
        for j in range(1, 5):
            P.stt(acc[:, :W], buf[i][:, j:j + W], cw[:, ch, j:j + 1], acc[:, :W], ALU.mult, ALU.add,
                  [tbuf[i], tc_, tacc], [tacc])
            if W == SEQ:
                yield
        P.actf(so[:, :W], acc[:, :W], AF.Silu, [tacc], [tso])
        if ch < 8:
            P.dma(k.XS[:, ch, c0:c0 + W], so[:, :W], reads=[tso], writes=[Tok()])
        else:
            P.copy(P.pool, sob[:, :W], so[:, :W], [tso], [tsob])
            P.dma(k.BCT[:, ch - 8, c0:c0 + W], sob[:, :W], reads=[tsob], writes=[Tok()], eng=P.pool)
        if 8 <= ch < 12:
            g = ch - 8
            nt = W // 128
            for t4 in range(0, nt, 4):
                pb = (t4 // 4) % 4
                m = min(4, nt - t4)
                for tt_ in range(m):
                    P.tr(k.ps[pb][:, tt_ * 128:(tt_ + 1) * 128], so[:, (t4 + tt_) * 128:(t4 + tt_ + 1) * 128],
                         k.ident[:], [tso, k.tconst], [k.tps[pb]])
                P.copy(P.act, obt[:, t4:t4 + m, :].rearrange("p a b -> p (a b)"), k.ps[pb][:, 0:m * 128],
                       [k.tps[pb]], [tobt])
            P.dma(k.BTOK[c0:c0 + W, g * 128:(g + 1) * 128].rearrange("(c p) n -> p c n", p=128),
                  obt[:, 0:nt, :], reads=[tobt], writes=[Tok()])
        yield

    def allsteps():
        for n in range(len(items)):
            yield from item(n)
    return allsteps()


def phase_ssd(k):
    P, dr, nc = k.P, k.dr, k.nc
    with ExitStack() as st:
        sb = lambda n, s, d: k.sb(n, s, d, st)
        masks = sb("s_masks", [128, 4, 128], F32)
        dskt = sb("s_dsk", [128, 2, 8], F32)
        dsum = sb("s_dsum", [128, 8], F32)
        gn = sb("s_gn", [128, 8], F32)
        tm = Tok()
        P.dma(masks[:], dr["masks"], writes=[tm])
        P.dma(dskt[:], dr["dsk"], writes=[tm])
        P.dma(gn[:], dr["gnorm"], writes=[tm])
        P.tt(P.dve, dsum[:], dskt[:, 0, :], dskt[:, 1, :], ALU.add, [tm], [tm])
        ACS = sb("s_acs", [128, 36, 32], F32)
        ATOT = sb("s_atot", [128, 36, 32], F32)
        EA = sb("s_ea", [128, 36, 32], F32)
        F1 = sb("s_f1", [128, 36, 32], F32)
        ETOT = sb("s_etot", [128, 36, 32], F32)
        tp = Tok()
        fl = lambda t: t[:].rearrange("p a b -> p (a b)")
        for g3 in range(3):
            q0 = g3 * 12
            for qi in range(12):
                q = q0 + qi
                P.mm(k.ps[0][:, qi * 32:qi * 32 + 16], masks[:, 0, :], k.ADTs[:, q, 0:16], True, True,
                     [tm, k.tdt], [k.tps[0]])
                P.mm(k.ps[0][:, qi * 32 + 16:qi * 32 + 32], masks[:, 1, :], k.ADTs[:, q, 16:32], True, True,
                     [tm, k.tdt], [k.tps[0]])
                P.mm(k.ps[1][:, qi * 32:qi * 32 + 32], k.ones[:], k.ADTs[:, q, :], True, True,
                     [k.tconst, k.tdt], [k.tps[1]])
            P.copy(P.dve, ACS[:, q0:q0 + 12, :].rearrange("p a b -> p (a b)"), k.ps[0][:, 0:384], [k.tps[0]], [tp])
            P.copy(P.act, ATOT[:, q0:q0 + 12, :].rearrange("p a b -> p (a b)"), k.ps[1][:, 0:384], [k.tps[1]], [tp])
        P.actf(fl(EA), fl(ACS), AF.Exp, [tp], [tp])
        P.tt(P.dve, fl(F1), fl(ATOT), fl(ACS), ALU.subtract, [tp], [tp])
        P.actf(fl(F1), fl(F1), AF.Exp, [tp], [tp])
        P.tt(P.dve, fl(F1), fl(F1), fl(k.DTs), ALU.mult, [tp, k.tdt], [tp])
        P.actf(fl(ETOT), fl(ATOT), AF.Exp, [tp], [tp])
        P.barrier()

        xsa = sb("s_xsa", [128, 2, 2304], F32)
        bta = sb("s_bta", [128, 2304], BF16)
        cta = sb("s_cta", [128, 2048], BF16)
        btk = sb("s_btk", [128, 18, 128], BF16)
        Yg = sb("s_yg", [128, 16, 256], F32)
        S = sb("s_S", [128, 256], F32)
        Sb = sb("s_Sb", [128, 256], BF16)
        class Set:
            pass
        sets = []
        for si in range(2):
            z = Set()
            z.xdd = sb(f"s_xdd{si}", [128, 256], BF16)
            z.xd = sb(f"s_xd{si}", [128, 256], BF16)
            z.Gm = sb(f"s_gm{si}", [128, 128], F32)
            z.lD = [sb(f"s_lD{si}_{i}", [128, 128], F32) for i in range(4)]
            z.E = sb(f"s_E{si}", [128, 512], F32)
            z.WT = sb(f"s_WT{si}", [128, 4, 128], BF16)
            z.yo = sb(f"s_yo{si}", [128, 256], F32)
            z.yo2 = sb(f"s_yo2{si}", [128, 256], F32)
            z.txdd, z.txd, z.tGm, z.tE, z.tWT, z.tyo, z.tyo2 = [Tok() for _ in range(7)]
            z.tlD = [Tok() for _ in range(4)]
            z.txdh = [Tok() for _ in range(4)]
            bb = 4 * si
            z.bx, z.bD, z.by, z.bs = k.ps[bb], k.ps[bb + 1], k.ps[bb + 2], k.ps[bb + 3]
            z.tx_, z.tG_, z.tD_, z.ty_, z.tyo_ = [Tok() for _ in range(5)]
            z.ts_ = k.tps[bb + 3]
            sets.append(z)
        tmpx = sb("s_tmpx", [128, 512], F32)
        szt = sb("s_szt", [128, 512], F32)
        yz = [sb(f"s_yz{i}", [128, 512], F32) for i in range(2)]
        ocat = sb("s_ocat", [128, 2, 512], BF16)
        tin, tY, tS, tSb = [Tok() for _ in range(4)]
        ttmpx, tszt, tocat = Tok(), Tok(), Tok()
        tyz = [Tok(), Tok()]
        h3 = lambda ap: ap.rearrange("p (h e) -> p h e", e=64)
        bc = lambda ap: ap.unsqueeze(2).to_broadcast([128, 4, 64])
        nx = 0
        for b in range(NB):
            lat0, cx0 = b * SEQ, NLAT + b * CTX
            for g in range(4):
                P.dma(xsa[:, :, 0:2048], k.XS[:, 2 * g:2 * g + 2, lat0:lat0 + 2048], writes=[tin])
                P.dma(xsa[:, :, 2048:2304], k.XS[:, 2 * g:2 * g + 2, cx0:cx0 + 256], writes=[tin])
                P.dma(bta[:, 0:2048], k.BCT[:, g, lat0:lat0 + 2048], writes=[tin])
                P.dma(bta[:, 2048:2304], k.BCT[:, g, cx0:cx0 + 256], writes=[tin])
                P.dma(cta[:], k.BCT[:, 4 + g, lat0:lat0 + 2048], writes=[tin])
                P.dma(btk[:, 0:16, :], k.BTOK[lat0:lat0 + 2048, g * 128:(g + 1) * 128].rearrange("(c p) n -> p c n", p=128),
                      writes=[tin])
                P.dma(btk[:, 16:18, :], k.BTOK[cx0:cx0 + 256, g * 128:(g + 1) * 128].rearrange("(c p) n -> p c n", p=128),
                      writes=[tin])
                for d in range(2):
                    order = ([16, 17] + list(range(16))) if d == 0 else ([17, 16] + list(range(15, -1, -1)))
                    P.memset(P.dve, S[:], 0.0, [tS])
                    P.memset(P.pool, Sb[:], 0.0, [tSb])
                    mR = masks[:, 0, :] if d == 0 else masks[:, 1, :]
                    mS = masks[:, 3, :] if d == 0 else masks[:, 2, :]
                    S1, S2 = [], []
                    for ci, cc in enumerate(order):
                        z = sets[nx % 2]
                        nx += 1

                        def s1(ci=ci, cc=cc, z=z, d=d, mR=mR, mS=mS):
                            lat = cc < 16
                            q = (b * 16 + cc) if lat else (32 + b * 2 + (cc - 16))
                            csl = slice(cc * 128, (cc + 1) * 128)
                            h0 = d * 16 + g * 4
                            hsl = slice(h0, h0 + 4)
                            P.tr(z.bx[:, 0:128], xsa[:, 0, csl], k.ident[:], [tin, k.tconst], [z.tx_, z.tG_])
                            P.tr(z.bx[:, 128:256], xsa[:, 1, csl], k.ident[:], [tin, k.tconst], [z.tx_, z.tG_])
                            if lat:
                                P.mm(z.bx[:, 256:384], bta[:, csl], cta[:, csl], True, True, [tin], [z.tG_])
                            P.tt(P.dve, h3(z.xdd[:]), h3(z.bx[:, 0:256]), bc(F1[:, q, hsl]), ALU.mult,
                                 [z.tx_, z.tG_, tp], [z.txdd])
                            if lat:
                                P.tt(P.dve, h3(z.xd[:]), h3(z.bx[:, 0:256]), bc(k.DTs[:, q, hsl]), ALU.mult,
                                     [z.tx_, z.tG_, k.tdt], [z.txd])
                                P.tt(P.dve, z.Gm[:], z.bx[:, 256:384], mR, ALU.mult, [z.tG_, tm], [z.tGm])
                                for hh in range(4):
                                    P.actf(z.lD[hh][:], mS, AF.Identity, [tm, k.tdt], [z.tlD[hh]],
                                           scale=k.ADTs[:, q, h0 + hh:h0 + hh + 1])
                                for hh in range(4):
                                    P.mm(z.bD[:, hh * 128:(hh + 1) * 128], z.lD[hh][:], mR, True, True,
                                         [z.tlD[hh], tm], [z.tD_])
                                P.actf(z.E[:], z.bD[:], AF.Exp, [z.tD_], [z.tE])
                                P.tt(P.pool, z.WT[:], z.E[:].rearrange("p (h s) -> p h s", s=128),
                                     z.Gm[:].unsqueeze(1).to_broadcast([128, 4, 128]), ALU.mult, [z.tE, z.tGm], [z.tWT])

                        def s2(ci=ci, cc=cc, z=z, d=d):
                            lat = cc < 16
                            q = (b * 16 + cc) if lat else (32 + b * 2 + (cc - 16))
                            csl = slice(cc * 128, (cc + 1) * 128)
                            h0 = d * 16 + g * 4
                            hsl = slice(h0, h0 + 4)
                            if lat:
                                for hh in range(4):
                                    P.mm(z.by[:, hh * 64:(hh + 1) * 64], z.WT[:, hh, :], z.xd[:, hh * 64:(hh + 1) * 64],
                                         True, True, [z.tWT, z.txd], [z.ty_])
                            if ci < 17:
                                P.mm(z.bs[:, 0:256], btk[:, cc, :], z.xdd[:], True, True, [tin, z.txdd], [z.ts_])
                            if lat:
                                P.mm(z.by[:, 256:512], cta[:, csl], Sb[:], True, True, [tin, tSb], [z.tyo_])
                            if ci < 17:
                                P.tt(P.dve, h3(S[:]), h3(S[:]), bc(ETOT[:, q, hsl]), ALU.mult, [tS, tp], [tS])
                                P.tt(P.dve, S[:], S[:], z.bs[:, 0:256], ALU.add, [tS, z.ts_], [tS])
                            if lat:
                                P.tt(P.dve, h3(z.yo[:]), h3(z.by[:, 256:512]), bc(EA[:, q, hsl]), ALU.mult,
                                     [z.tyo_, tp], [z.tyo])
                            if ci < 17:
                                P.copy(P.dve, Sb[:], S[:], [tS], [tSb])
                            if lat:
                                if d == 0:
                                    P.tt(P.dve, Yg[:, cc, :], z.yo[:], z.by[:, 0:256], ALU.add, [z.tyo, z.ty_], [tY])
                                else:
                                    P.tt(P.dve, z.yo2[:], z.yo[:], z.by[:, 0:256], ALU.add, [z.tyo, z.ty_], [z.tyo2])
                                    P.tt(P.pool, Yg[:, cc, :], Yg[:, cc, :], z.yo2[:], ALU.add, [z.tyo2, tY], [tY])
                        S1.append(s1)
                        S2.append(s2)
                    for i_ in range(len(S1) + 1):
                        if i_ < len(S1):
                            S1[i_]()
                        if i_ >= 1:
                            S2[i_ - 1]()
                for qt in range(4):
                    for j in range(2):
                        pb = 2 + j
                        ptk = [sets[0].ty_, sets[0].tyo_] if j == 0 else [sets[0].ts_]
                        for t4 in range(4):
                            P.tr(k.ps[pb][:, t4 * 128:(t4 + 1) * 128], Yg[:, qt * 4 + t4, j * 128:(j + 1) * 128],
                                 k.ident[:], [tY, k.tconst], ptk)
                        P.stt(tmpx[:], xsa[:, j, qt * 512:(qt + 1) * 512], dsum[:, 2 * g + j:2 * g + j + 1],
                              k.ps[pb][:], ALU.mult, ALU.add, [tin, tm] + ptk, [ttmpx])
                        P.dma(szt[:], k.SZ[:, 2 * g + j, lat0 + qt * 512:lat0 + (qt + 1) * 512], writes=[tszt])
                        P.tt(P.dve, yz[j][:], tmpx[:], szt[:], ALU.mult, [ttmpx, tszt], [tyz[j]])
                    normT(k, [yz[0][:], yz[1][:]], 256, 512, [ocat[:, 0, :], ocat[:, 1, :]],
                          A=[gn[:, 2 * g:2 * g + 1], gn[:, 2 * g + 1:2 * g + 2]], rd=[tyz[0], tyz[1], tm], wr=[tocat])
                    P.dma(k.CAT[:, 2 * g:2 * g + 2, lat0 + qt * 512:lat0 + (qt + 1) * 512], ocat[:],
                          reads=[tocat], writes=[Tok()])
        P.barrier()


def phase_na(k):
    P, dr, nc = k.P, k.dr, k.nc
    with ExitStack() as st:
        sb = lambda n, s, d: k.sb(n, s, d, st)
        nkb = sb("n_nk", [128, 2304], BF16)
        nvp = sb("n_nv", [128, 18, 128], BF16)
        nqb = sb("n_nq", [128, 2048], BF16)
        qm = sb("n_qm", [128, 2048], BF16)
        tab = sb("n_tab", [128, 25, 128], F32)
        sa = [sb(f"n_sa{i}", [128, 512], F32) for i in range(2)]
        s4 = [sb(f"n_s4{i}", [128, 128], F32) for i in range(2)]
        pA = [sb(f"n_pA{i}", [128, 512], BF16) for i in range(2)]
        pB = [sb(f"n_pB{i}", [128, 384], BF16) for i in range(2)]
        rc = sb("n_rc", [128, 512], F32)
        ocat = sb("n_ocat", [128, 2048], BF16)
        tin, tq, tqm, ttab, trc, tocat = [Tok() for _ in range(6)]
        tsa, ts4, tpA, tpB = [Tok(), Tok()], [Tok(), Tok()], [Tok(), Tok()], [Tok(), Tok()]
        cv = conv_setup(k, st)
        cvs = {"done": False}

        def conv_next(nsteps=1):
            for _ in range(nsteps):
                if cvs["done"]:
                    return
                try:
                    next(cv)
                except StopIteration:
                    cvs["done"] = True

        n = 0
        for b in range(NB):
            lat0, cx0 = b * SEQ, NLAT + b * CTX
            for hp in range(4):
                P.dma(nkb[:, 0:2048], k.NK[:, hp, lat0:lat0 + 2048], writes=[tin])
                P.dma(nkb[:, 2048:2304], k.NK[:, hp, cx0:cx0 + 256], writes=[tin])
                P.dma(nvp[:, 0:16, :], k.NV[lat0:lat0 + 2048, hp * 128:(hp + 1) * 128].rearrange("(c p) e -> p c e", p=128),
                      writes=[tin])
                P.dma(nvp[:, 16:18, :], k.NV[cx0:cx0 + 256, hp * 128:(hp + 1) * 128].rearrange("(c p) e -> p c e", p=128),
                      writes=[tin])
                P.dma(nqb[:], k.NQ[:, hp, lat0:lat0 + 2048], writes=[tq])
                for half in range(2):
                    h = 2 * hp + half
                    hs = slice(half * 64, half * 64 + 64)
                    os_ = slice((1 - half) * 64, (1 - half) * 64 + 64)
                    P.dma(tab[:], dr["nab"][h], writes=[ttab])
                    P.copy(P.dve, qm[hs, :], nqb[hs, :], [tq], [tqm])
                    P.memset(P.dve, qm[os_, :], 0.0, [tqm])
                    A, B = [], []
                    for qb in range(16):
                        i = n % 2
                        n += 1

                        def a_(qb=qb, i=i):
                            start = min(max(qb - 2, 0), 11)
                            cls = {0: 0, 1: 1, 14: 3, 15: 4}.get(qb, 2)
                            bA, bB = i, 2 + i
                            qsl = slice(qb * 128, (qb + 1) * 128)
                            for c5 in range(5):
                                dst = k.ps[bA][:, c5 * 128:(c5 + 1) * 128] if c5 < 4 else k.ps[bB][:, 0:128]
                                tk_ = k.tps[bA] if c5 < 4 else k.tps[bB]
                                P.mm(dst, nkb[:, (start + c5) * 128:(start + c5 + 1) * 128], qm[:, qsl], True, True,
                                     [tin, tqm], [tk_])
                            for cx in range(2):
                                P.mm(k.ps[bB][:, 128 + cx * 128:256 + cx * 128], nkb[:, 2048 + cx * 128:2176 + cx * 128],
                                     qm[:, qsl], True, True, [tin, tqm], [k.tps[bB]])
                            P.stt(sa[i][:], k.ps[bA][:], 0.125, tab[:, cls * 5:cls * 5 + 4, :].rearrange("p a b -> p (a b)"),
                                  ALU.mult, ALU.add, [k.tps[bA], ttab], [tsa[i]])
                            P.stt(s4[i][:], k.ps[bB][:, 0:128], 0.125, tab[:, cls * 5 + 4, :], ALU.mult, ALU.add,
                                  [k.tps[bB], ttab], [ts4[i]])
                            P.actf(pA[i][:], sa[i][:], AF.Exp, [tsa[i]], [tpA[i]])
                            P.actf(pB[i][:, 0:128], s4[i][:], AF.Exp, [ts4[i]], [tpB[i]])
                            P.actf(pB[i][:, 128:384], k.ps[bB][:, 128:384], AF.Exp, [k.tps[bB]], [tpB[i]], scale=0.125)

                        def b_(qb=qb, i=i, hs=hs):
                            start = min(max(qb - 2, 0), 11)
                            po, pd = (4, 5) if (qb // 4) % 2 == 0 else (6, 7)
                            osl = slice((qb % 4) * 128, (qb % 4 + 1) * 128)
                            for c7 in range(7):
                                if c7 < 4:
                                    rhs, tr_ = pA[i][:, c7 * 128:(c7 + 1) * 128], tpA[i]
                                    kc = start + c7
                                elif c7 == 4:
                                    rhs, tr_ = pB[i][:, 0:128], tpB[i]
                                    kc = start + 4
                                else:
                                    rhs, tr_ = pB[i][:, 128 + (c7 - 5) * 128:256 + (c7 - 5) * 128], tpB[i]
                                    kc = 16 + (c7 - 5)
                                P.mm(k.ps[po][:, osl], nvp[:, kc, :], rhs, c7 == 0, c7 == 6, [tin, tr_], [k.tps[po]])
                                P.mm(k.ps[pd][:, osl], k.onesb[:], rhs, c7 == 0, c7 == 6, [k.tconst, tr_], [k.tps[pd]])
                            if qb % 4 == 3:
                                o0 = (qb // 4) * 512
                                P.actf(rc[hs, :], k.ps[pd][hs, :], AF.Ln, [k.tps[pd]], [trc])
                                P.actf(rc[hs, :], rc[hs, :], AF.Exp, [trc], [trc], scale=-1.0)
                                P.tt(P.dve, ocat[hs, o0:o0 + 512], k.ps[po][hs, :], rc[hs, :], ALU.mult,
                                     [k.tps[po], trc], [tocat])
                        A.append(a_)
                        B.append(b_)
                    for qi in range(17):
                        if qi < 16:
                            A[qi]()
                        if qi >= 1:
                            B[qi - 1]()
                        conv_next(2 if qi % 2 else 1)
                P.dma(k.CAT[:, 8 + hp, lat0:lat0 + 2048], ocat[:], reads=[tocat], writes=[Tok()])
        while not cvs["done"]:
            conv_next()
        P.barrier()


_CACHE = {}


def kernel(**inputs):
    sh = prep_shared(inputs)
    cores = [prep_core(inputs, c) for c in range(DBG["ncores"])]
    shapes = {n: a.shape for n, a in sh.items()}
    shapes.update({n: a.shape for n, a in cores[0].items()})
    nc = build(shapes)
    in_maps = []
    ncr = DBG["ncores"]
    for c in range(ncr):
        m = dict(sh)
        m.update(cores[c])
        in_maps.append(m)
    res = run_bass_kernel_spmd(nc, in_maps, core_ids=list(range(ncr)))
    if DBG["stop"] is not None:
        return [r["XTd"] for r in res.results]
    outs = []
    for c in range(NCORES):
        o = res.results[c]["oT"]
        o = o.transpose(2, 1, 0).reshape(NB, SEQ, D)
        outs.append(o)
    return np.ascontiguousarray(np.concatenate(outs, 0).astype(np.float32))
```

```python
import numpy as np
import ml_dtypes
from contextlib import ExitStack
import concourse.bass as bass
import concourse.mybir as mybir
from concourse.bass_utils import run_bass_kernel_spmd

F32 = mybir.dt.float32
BF16 = mybir.dt.bfloat16
AF = mybir.ActivationFunctionType
ALU = mybir.AluOpType

NCORES = 8
NB = 2
D = 1024
SEQ = 2048
CTX = 256
NLAT = NB * SEQ
NCOL = NLAT + NB * CTX
FH = 2816
EPS = 1e-6
LAMBDA_INIT0 = 0.2
GRID_W = 64

DBG = {"stop": None, "ncores": NCORES}


class Tok:
    __slots__ = ("w", "r")

    def __init__(self):
        self.w = None
        self.r = {}


class Eng:
    def __init__(self, name, e, inorder_safe=False):
        self.name, self.e = name, e
        self.key = "s_" + name
        self.count = 0
        self.seen = {}
        self.inorder_safe = inorder_safe


class Prog:
    def __init__(self, nc, stack, n_dma_sems=32):
        self.nc = nc
        self.sems = {}

        def mk(name):
            self.sems[name] = stack.enter_context(nc.semaphore(name))

        self.pe = Eng("pe", nc.tensor, True)
        self.act = Eng("act", nc.scalar)
        self.dve = Eng("dve", nc.vector)
        self.pool = Eng("pool", nc.gpsimd)
        self.sp = Eng("sp", nc.sync)
        self.engs = [self.pe, self.act, self.dve, self.pool, self.sp]
        for e in self.engs:
            mk(e.key)
        self.dma_pools = {}
        for e, n in ((self.sp, 24), (self.pool, 8), (self.act, 8)):
            lst = []
            for i in range(n):
                mk(f"s_dma_{e.name}{i}")
                lst.append([f"s_dma_{e.name}{i}", 0])
            self.dma_pools[e.name] = [lst, 0]
        self.n_inst = 0

    def _need(self, eng, waits, ev):
        if ev is None:
            return
        k, v = ev
        if eng.seen.get(k, 0) >= v:
            return
        if waits.get(k, 0) < v:
            waits[k] = v

    def _deps(self, eng, reads, writes, extra=None):
        waits = {}
        if extra is not None:
            self._need(eng, waits, extra)
        for t in reads:
            self._need(eng, waits, t.w)
        for t in writes:
            self._need(eng, waits, t.w)
            for k, v in t.r.items():
                self._need(eng, waits, (k, v))
        if eng.inorder_safe and eng.key in waits:
            del waits[eng.key]
        for k, v in waits.items():
            eng.e.wait_ge(self.sems[k], v)
            eng.seen[k] = v

    def _mark(self, ev, reads, writes):
        k, v = ev
        for t in reads:
            if t.r.get(k, 0) < v:
                t.r[k] = v
        for t in writes:
            t.w = ev
            t.r = {}

    def op(self, eng, fn, reads=(), writes=()):
        self._deps(eng, reads, writes)
        inst = fn(eng.e)
        eng.count += 1
        inst.then_inc(self.sems[eng.key], 1)
        self._mark((eng.key, eng.count), reads, writes)
        self.n_inst += 1

    def dma(self, out, in_, reads=(), writes=(), eng=None, **kw):
        eng = eng or self.sp
        pl = self.dma_pools[eng.name]
        slot = pl[0][pl[1] % len(pl[0])]
        pl[1] += 1
        k = slot[0]
        prev = (k, slot[1]) if slot[1] > 0 else None
        self._deps(eng, reads, writes, extra=prev)
        inst = eng.e.dma_start(out=out, in_=in_, **kw)
        slot[1] += 16
        inst.then_inc(self.sems[k], 16)
        self._mark((k, slot[1]), reads, writes)
        self.n_inst += 1

    def barrier(self):
        evs = [(e.key, e.count) for e in self.engs if e.count > 0]
        for pl in self.dma_pools.values():
            evs += [(s[0], s[1]) for s in pl[0] if s[1] > 0]
        for e in self.engs:
            for k, v in evs:
                if k == e.key:
                    continue
                if e.seen.get(k, 0) < v:
                    e.e.wait_ge(self.sems[k], v)
                    e.seen[k] = v

    def mm(self, out, lhsT, rhs, start, stop, reads, writes):
        self.op(self.pe, lambda e: e.matmul(out, lhsT=lhsT, rhs=rhs, start=start, stop=stop),
                reads, writes)

    def tr(self, out, in_, ident, reads, writes):
        self.op(self.pe, lambda e: e.transpose(out, in_, ident), reads, writes)

    def actf(self, out, in_, func, reads, writes, scale=1.0, bias=0.0):
        self.op(self.act, lambda e: e.activation(out=out, in_=in_, func=func, bias=bias, scale=scale),
                reads, writes)

    def tt(self, eng, out, in0, in1, op, reads, writes):
        self.op(eng, lambda e: e.tensor_tensor(out=out, in0=in0, in1=in1, op=op), reads, writes)

    def ts(self, eng, out, in0, s1, op0, reads, writes, s2=None, op1=None):
        if op1 is None:
            self.op(eng, lambda e: e.tensor_scalar(out=out, in0=in0, scalar1=s1, scalar2=None, op0=op0),
                    reads, writes)
        else:
            self.op(eng, lambda e: e.tensor_scalar(out=out, in0=in0, scalar1=s1, scalar2=s2, op0=op0, op1=op1),
                    reads, writes)

    def stt(self, out, in0, scalar, in1, op0, op1, reads, writes):
        self.op(self.dve, lambda e: e.scalar_tensor_tensor(out=out, in0=in0, scalar=scalar, in1=in1,
                                                           op0=op0, op1=op1), reads, writes)

    def copy(self, eng, out, in_, reads, writes):
        if eng is self.act:
            self.op(eng, lambda e: e.copy(out=out, in_=in_), reads, writes)
        else:
            self.op(eng, lambda e: e.tensor_copy(out=out, in_=in_), reads, writes)

    def memset(self, eng, ap, val, writes):
        self.op(eng, lambda e: e.memset(ap, val), (), writes)


def fm(v, n):
    return np.ascontiguousarray(np.asarray(v, np.float32).reshape(n, 128).T)


def wk(w):
    K = w.shape[0] // 128
    return np.ascontiguousarray(w.reshape(K, 128, w.shape[1]).transpose(1, 0, 2))


def rope_perm64():
    i = np.arange(64)
    half = (i % 32) // 16
    return np.where(half == 0, i + 16, i - 16)


def rope_tables():
    t = np.arange(SEQ)
    row = (t // GRID_W).astype(np.float32)
    col = (t % GRID_W).astype(np.float32)
    nf = 16
    freqs = (np.float32(10000.0) ** (-np.arange(nf, dtype=np.float32) / np.float32(nf))).astype(np.float32)
    i = np.arange(64)
    a = i // 32
    half = (i % 32) // 16
    f = i % 16
    pos = np.where(a[:, None] == 0, row[None, :], col[None, :]).astype(np.float32)
    ang = (pos * freqs[f][:, None]).astype(np.float32)
    cos = np.cos(ang).astype(np.float32)
    sin = np.sin(ang).astype(np.float32)
    sgn = np.where(half == 0, -1.0, 1.0).astype(np.float32)[:, None]
    sins = (sin * sgn).astype(np.float32)
    return np.concatenate([cos, cos], 0), np.concatenate([sins, sins], 0)


def prep_shared(inp):
    sh = {}
    f32 = np.float32
    win = np.empty((4, 22, 128, 8, 256), f32)
    wout = np.empty((4, 8, 128, 22, 128), f32)
    for l in range(2):
        for wi, (a, b) in enumerate((("w_ffn1_in", "w_ffn1_out"), ("w_ffn2_in", "w_ffn2_out"))):
            w = np.asarray(inp[a][l], f32)
            wkk = w.reshape(8, 128, 5632).transpose(1, 0, 2)
            g = wkk[:, :, :FH].reshape(128, 8, 22, 128)
            u = wkk[:, :, FH:].reshape(128, 8, 22, 128)
            fi = l * 2 + wi
            win[fi, :, :, :, :128] = g.transpose(2, 0, 1, 3)
            win[fi, :, :, :, 128:] = u.transpose(2, 0, 1, 3)
            wo = np.asarray(inp[b][l], f32)
            wout[fi] = wo.reshape(22, 128, 8, 128).transpose(2, 1, 0, 3)
    sh["win"] = win
    sh["wout"] = wout
    wm = np.asarray(inp["w_mod"], f32).reshape(2, 8, 128, 36, 2, 128)
    sh["wmod"] = np.ascontiguousarray(wm.transpose(0, 3, 2, 4, 1, 5))
    sh["bmod"] = np.ascontiguousarray(np.asarray(inp["b_mod"], f32).reshape(2, 72, 128).transpose(2, 0, 1))
    gs = [inp["g_ffn1"][0], inp["g_mix"][0], inp["g_ffn2"][0], inp["g_ffn1"][1], inp["g_mix"][1],
          inp["g_ffn2"][1], inp["g_final"]]
    sh["gains"] = np.ascontiguousarray(np.stack([fm(g, 8) for g in gs], 1))
    sh["ident"] = np.eye(128, dtype=f32)
    sh["ones"] = np.ones((128, 128), f32)
    w = np.asarray(inp["ab_w_in"][0], f32)
    p64 = rope_perm64()
    p512 = (np.arange(512) // 64) * 64
    p512 = p512 + p64[np.arange(512) % 64]
    qd, kd, vd = w[:, 0:512], w[:, 512:1024], w[:, 1024:1536]
    cq, ckv, kr = w[:, 1536:1792], w[:, 1792:1920], w[:, 1920:1984]
    cols = np.concatenate([qd, qd[:, p512], kd, kd[:, p512], vd, cq, ckv,
                           kr, kr, kr[:, p64], kr[:, p64]], 1)
    assert cols.shape[1] == 3200
    sh["wab"] = wk(cols)
    wuq = np.asarray(inp["ab_w_uq"][0], f32)
    z64 = np.zeros((256, 64), f32)
    qn = [wuq[:, h * 192:h * 192 + 128] for h in range(4)]
    qr = [np.concatenate([wuq[:, h * 192 + 128:h * 192 + 192], z64], 1) for h in range(4)]
    qrp = [np.concatenate([wuq[:, h * 192 + 128:h * 192 + 192][:, p64], z64], 1) for h in range(4)]
    sh["wuq"] = wk(np.concatenate(qn + qr + qrp, 1))
    wukv = np.asarray(inp["ab_w_ukv"][0], f32)
    kn = [wukv[:, h * 256:h * 256 + 128] for h in range(4)]
    vv = [wukv[:, h * 256 + 128:h * 256 + 256] for h in range(4)]
    sh["wukv"] = np.ascontiguousarray(np.concatenate(kn + vv, 1))
    sh["wabo"] = wk(np.asarray(inp["ab_w_out"][0], f32))
    sh["abvec"] = np.ascontiguousarray(np.concatenate(
        [fm(inp["ab_g_q"][0], 2), fm(inp["ab_g_kv"][0], 1), fm(inp["ab_g_subln"][0], 1)], 1))
    sh["lamv"] = np.ascontiguousarray(np.stack(
        [inp["ab_lam_q1"][0], inp["ab_lam_k1"][0], inp["ab_lam_q2"][0], inp["ab_lam_k2"][0]], 0).astype(f32))
    cos, sins = rope_tables()
    sh["cos"] = cos
    sh["sins"] = sins
    w = np.asarray(inp["cd_w_in"][0], f32)
    zc, xbc, dtc = w[:, 0:1024], w[:, 1024:3072], w[:, 3072:3104]
    qc, kc, vc = w[:, 3104:3616], w[:, 3616:4128], w[:, 4128:4640]
    cols = np.concatenate([zc, xbc, qc, kc, vc, dtc, np.zeros((1024, 96), f32)], 1)
    sh["wcd"] = wk(cols)
    cw = np.asarray(inp["cd_conv_w"][0], f32)
    sh["convw"] = np.ascontiguousarray(cw.reshape(5, 16, 128).transpose(2, 1, 0))
    sh["convb"] = fm(inp["cd_conv_b"][0], 16)
    sh["dtb"] = np.ascontiguousarray(np.asarray(inp["cd_dt_bias"][0], f32).reshape(1, 32))
    sh["alog"] = np.ascontiguousarray(np.asarray(inp["cd_a_log"][0], f32).reshape(1, 32))
    dsk = np.asarray(inp["cd_d_skip"][0], f32)
    hidx = (np.arange(1024) // 64)
    sh["dsk"] = np.ascontiguousarray(np.stack([fm(dsk[0][hidx], 8), fm(dsk[1][hidx], 8)], 1))
    sh["gnorm"] = fm(inp["cd_g_norm"][0], 8)
    sh["wcdo"] = wk(np.asarray(inp["cd_w_out"][0], f32))
    sh["nab"] = na_tables(np.asarray(inp["cd_rpb"][0], f32))
    u = np.arange(128)
    U = (u[:, None] <= u[None, :]).astype(f32)
    L = (u[:, None] >= u[None, :]).astype(f32)
    Us = (u[:, None] < u[None, :]).astype(f32)
    Ls = (u[:, None] > u[None, :]).astype(f32)
    sh["masks"] = np.ascontiguousarray(np.stack([U, L, Us, Ls], 1))
    return sh


def na_tables(rpb):
    NEG = np.float32(-30000.0)
    tab = np.full((8, 128, 25, 128), NEG, np.float32)
    ki = np.arange(128)
    qi = np.arange(128)
    for cls, qb in enumerate((0, 1, 7, 14, 15)):
        start = min(max(qb - 2, 0), 11)
        for i in range(5):
            m = start + i
            kr = 2 * m + ki // 64
            kc = ki % 64
            r = 2 * qb + qi // 64
            c = qi % 64
            r0 = np.clip(r - 4, 0, 32 - 8)
            c0 = np.clip(c - 8, 0, 64 - 16)
            valid = ((kr[:, None] >= r0[None, :]) & (kr[:, None] < r0[None, :] + 8) &
                     (kc[:, None] >= c0[None, :]) & (kc[:, None] < c0[None, :] + 16))
            dr_ = np.clip(kr[:, None] - r[None, :] + 7, 0, 14)
            dc_ = np.clip(kc[:, None] - c[None, :] + 15, 0, 30)
            for h in range(8):
                tab[h, :, cls * 5 + i, :] = np.where(valid, rpb[h][dr_, dc_], NEG)
    return tab


def prep_core(inp, core):
    pc = {}
    b0 = core * NB
    rows = np.concatenate([np.asarray(inp["x"][b0 + b], np.float32) for b in range(NB)] +
                          [np.asarray(inp["ctx"][b0 + b], np.float32) for b in range(NB)], 0)
    pc["xT"] = np.ascontiguousarray(rows.reshape(NCOL, 8, 128).transpose(2, 1, 0))
    cc = np.zeros((4, D), np.float32)
    cc[0] = inp["c"][b0]
    cc[1] = inp["c"][b0 + 1]
    cc[2] = inp["c_ctx"]
    pc["cT"] = np.ascontiguousarray(cc.reshape(4, 8, 128).transpose(2, 1, 0))
    return pc


class K:
    pass


def build(shapes):
    nc = bass.Bass("TRN2", target_bir_lowering=False)
    k = K()
    k.nc = nc
    dr = {}
    for name, shp in shapes.items():
        dr[name] = nc.dram_tensor(name, list(shp), F32, kind="ExternalInput").ap()
    k.dr = dr
    k.oT = nc.dram_tensor("oT", [128, 8, NLAT], F32, kind="ExternalOutput").ap()
    if DBG["stop"] is not None:
        k.XT = nc.dram_tensor("XTd", [128, 8, NCOL], F32, kind="ExternalOutput").ap()
    else:
        k.XT = nc.dram_tensor("XTs", [128, 8, NCOL], F32, kind="Internal").ap()

    def scratch(name, shp, dt):
        return nc.dram_tensor(name, list(shp), dt, kind="Internal").ap()

    k.QD = scratch("QD", [128, 4, NCOL], BF16)
    k.KD = scratch("KD", [128, 4, NCOL], BF16)
    k.VD = scratch("VD", [NCOL, 512], BF16)
    k.QN = scratch("QN", [128, 4, NCOL], BF16)
    k.QR = scratch("QR", [128, 4, NCOL], BF16)
    k.KN = scratch("KN", [128, 4, NCOL], BF16)
    k.KR = scratch("KR", [128, NCOL], BF16)
    k.VM = scratch("VM", [NCOL, 512], BF16)
    k.CAT = scratch("CAT", [128, 12, NCOL], BF16)
    k.WBI = scratch("WBI", [4, 22, 128, 8, 256], BF16)
    k.WBO = scratch("WBO", [4, 8, 128, 22, 128], BF16)
    k.tWBI = [[Tok() for _ in range(22)] for _ in range(4)]
    k.tWBO = [[Tok() for _ in range(8)] for _ in range(4)]
    k.SZ = scratch("SZ", [128, 8, NLAT], F32)
    k.XBC = scratch("XBC", [128, 16, NCOL], F32)
    k.XS = scratch("XS", [128, 8, NCOL], F32)
    k.BCT = scratch("BCT", [128, 8, NCOL], BF16)
    k.BTOK = scratch("BTOK", [NCOL, 512], BF16)
    k.NQ = scratch("NQ", [128, 4, NCOL], BF16)
    k.NK = scratch("NK", [128, 4, NCOL], BF16)
    k.NV = scratch("NV", [NCOL, 512], BF16)

    with ExitStack() as st:
        P = Prog(nc, st)
        k.P = P
        k.st = st

        k.uid = 0

        def sb(name, shp, dt, stack=st):
            k.uid += 1
            return stack.enter_context(nc.sbuf_tensor(f"sb{k.uid}_{name}", list(shp), dt))

        k.sb = sb
        k.ps = [st.enter_context(nc.psum_tensor(f"ps{i}", [128, 512], F32)) for i in range(8)]
        k.tps = [Tok() for _ in range(8)]
        k.ident = sb("ident", [128, 128], F32)
        k.ones = sb("ones", [128, 128], F32)
        k.onesb = sb("onesb", [128, 128], BF16)
        k.tconst = Tok()
        P.dma(k.ident[:], dr["ident"], writes=[k.tconst])
        P.dma(k.ones[:], dr["ones"], writes=[k.tconst])
        P.copy(P.dve, k.onesb[:], k.ones[:], [k.tconst], [k.tconst])
        k.mods = sb("mods", [128, 2, 9, 8, 4], F32)
        k.tmods = Tok()
        k.gains = sb("gains", [128, 7, 8], F32)
        P.dma(k.gains[:], dr["gains"], writes=[k.tconst])
        k.nsq = [sb(f"nsq{i}", [128, 512], BF16) for i in range(2)]
        k.tnsq = [Tok() for _ in range(2)]
        k.nln = sb("nln", [128, 512], F32)
        k.tnln = Tok()
        k.nrs = sb("nrs", [128, 512], F32)
        k.tnrs = Tok()
        k.ntmp = [sb(f"ntmp{i}", [128, 512], F32) for i in range(2)]
        k.tntmp = [Tok() for _ in range(2)]
        k.rr = 0

        precast(k, [0], w_out=False)
        phase_mod(k)
        P.barrier()
        precast(k, [0], w_in=False)
        precast_small(k, ["wab", "wuq", "wukv"])
        if DBG["stop"] == "mod":
            return finish(k)
        phase_ffn(k, 0, 0, True, src=dr["xT"])
        P.barrier()
        if DBG["stop"] == "ffn1_0":
            return finish(k)
        phase_mix_ab(k)
        P.barrier()
        precast_small(k, ["wabo"])
        precast(k, [1])
        phase_att_ab(k)
        P.barrier()
        if DBG["stop"] == "mix0":
            phase_outproj(k, 0, "wabo", 8, True)
            return finish(k)
        phase_outproj(k, 0, "wabo", 8, True)
        precast_small(k, ["wcd", "wcdo"])
        precast(k, [2])
        phase_ffn(k, 0, 1, True)
        P.barrier()
        if DBG["stop"] == "l0":
            return finish(k)
        precast(k, [3])
        phase_ffn(k, 1, 0, True)
        P.barrier()
        if DBG["stop"] == "ffn1_1":
            return finish(k)
        k.DTs = sb("DTs", [128, 36, 32], F32)
        k.ADTs = sb("ADTs", [128, 36, 32], F32)
        k.tdt = Tok()
        phase_mix_cd(k)
        phase_na(k)
        phase_ssd(k)
        phase_outproj(k, 1, "wcdo", 12, False)
        if DBG["stop"] == "mix1":
            return finish(k)
        phase_ffn(k, 1, 1, False, final=True)
        P.barrier()
        return finish(k)


def finish(k):
    k.P.barrier()
    return k.nc


def normT(k, xin, nfeat, W, outs, A=None, B=None, rd=(), wr=()):
    P = k.P
    n = len(xin)
    pb = 7
    for c in range(n):
        i = k.rr % 2
        k.rr += 1
        P.actf(k.nsq[i][:, :W], xin[c], AF.Square, list(rd), [k.tnsq[i]])
        P.mm(k.ps[pb][:, :W], k.onesb[:], k.nsq[i][:, :W], c == 0, c == n - 1,
             [k.tnsq[i], k.tconst], [k.tps[pb]])
    P.actf(k.nln[:, :W], k.ps[pb][:, :W], AF.Ln, [k.tps[pb]], [k.tnln], scale=1.0 / nfeat, bias=EPS)
    P.actf(k.nrs[:, :W], k.nln[:, :W], AF.Exp, [k.tnln], [k.tnrs], scale=-0.5)
    for c in range(n):
        if B is None:
            if A is None:
                P.tt(P.dve, outs[c], xin[c], k.nrs[:, :W], ALU.mult, list(rd) + [k.tnrs], list(wr))
            else:
                P.stt(outs[c], xin[c], A[c], k.nrs[:, :W], ALU.mult, ALU.mult,
                      list(rd) + [k.tnrs, k.tmods], list(wr))
        else:
            i = k.rr % 2
            k.rr += 1
            P.stt(k.ntmp[i][:, :W], xin[c], A[c], k.nrs[:, :W], ALU.mult, ALU.mult,
                  list(rd) + [k.tnrs, k.tmods], [k.tntmp[i]])
            P.actf(outs[c], k.ntmp[i][:, :W], AF.Identity, [k.tntmp[i], k.tmods], list(wr), bias=B[c])


def phase_mod(k):
    P, dr, nc = k.P, k.dr, k.nc
    with ExitStack() as st:
        sb = lambda n, s, d: k.sb(n, s, d, st)
        cT = sb("cT", [128, 8, 4], F32)
        sc = sb("scT", [128, 8, 4], F32)
        bm = sb("bmodT", [128, 2, 72], F32)
        tc_, tb = Tok(), Tok()
        P.dma(cT[:], dr["cT"], writes=[tc_])
        P.dma(bm[:], dr["bmod"], writes=[tb])
        P.actf(sc[:], cT[:], AF.Silu, [tc_], [tc_])
        wst = [sb(f"wmst{i}", [128, 2, 8, 128], F32) for i in range(3)]
        tw = [Tok() for _ in range(3)]
        for l in range(2):
            pb = l
            for g in range(36):
                i = g % 3
                P.dma(wst[i][:], dr["wmod"][l, g], writes=[tw[i]])
                for j2 in range(2):
                    jc = g * 2 + j2
                    for kk in range(8):
                        P.mm(k.ps[pb][:, jc * 4:jc * 4 + 4], wst[i][:, j2, kk, :], sc[:, kk, :],
                             kk == 0, kk == 7, [tw[i], tc_], [k.tps[pb]])
            P.tt(P.dve, k.mods[:, l].rearrange("p j c r -> p (j c) r"),
                 k.ps[pb][:, 0:288].rearrange("p (a r) -> p a r", r=4),
                 bm[:, l, :].unsqueeze(2).to_broadcast([128, 72, 4]), ALU.add,
                 [k.tps[pb], tb], [k.tmods])
        for l in range(2):
            for si, j in enumerate((1, 4, 7)):
                gv = k.gains[:, l * 3 + si, :].unsqueeze(2).to_broadcast([128, 8, 4])
                P.ts(P.dve, k.mods[:, l, j], k.mods[:, l, j], 1.0, ALU.add, [k.tmods], [k.tmods])
                P.tt(P.dve, k.mods[:, l, j], k.mods[:, l, j], gv, ALU.mult, [k.tmods, k.tconst], [k.tmods])
            for j in (2, 8):
                P.ts(P.dve, k.mods[:, l, j], k.mods[:, l, j], 0.5, ALU.mult, [k.tmods], [k.tmods])
        P.barrier()


def precast(k, fis, w_in=True, w_out=True):
    P, dr = k.P, k.dr
    for fi in fis:
        if w_in:
            for fp in range(22):
                P.dma(k.WBI[fi, fp], dr["win"][fi, fp], writes=[k.tWBI[fi][fp]], eng=P.pool, max_dma_last_dim=4096)
        if w_out:
            for dc in range(8):
                P.dma(k.WBO[fi, dc], dr["wout"][fi, dc], writes=[k.tWBO[fi][dc]], eng=P.pool, max_dma_last_dim=4096)


def precast_small(k, names):
    P, dr, nc = k.P, k.dr, k.nc
    if not hasattr(k, "wbf"):
        k.wbf = {}
        k.twbf = {}
    for name in names:
        src = dr[name]
        shp = list(src.shape)
        dst = nc.dram_tensor("bf_" + name, shp, BF16, kind="Internal").ap()
        k.wbf[name] = dst
        k.twbf[name] = Tok()
        if len(shp) == 3:
            toks = []
            for kk in range(shp[1]):
                t = Tok()
                P.dma(dst[:, kk, :], src[:, kk, :], writes=[t], eng=P.pool, max_dma_last_dim=4096)
                toks.append(t)
            k.twbf[name] = toks
        else:
            t = Tok()
            P.dma(dst, src, writes=[t], eng=P.pool, max_dma_last_dim=4096)
            k.twbf[name] = [t]


def norm_a(k, xin, W, pb, rd):
    P = k.P
    n = len(xin)
    for c in range(n):
        i = k.rr % 2
        k.rr += 1
        P.actf(k.nsq[i][:, :W], xin[c], AF.Square, list(rd), [k.tnsq[i]])
        P.mm(k.ps[pb][:, :W], k.onesb[:], k.nsq[i][:, :W], c == 0, c == n - 1,
             [k.tnsq[i], k.tconst], [k.tps[pb]])


def norm_b(k, xin, nfeat, W, pb, outs, A, B, rs, trs, rd, wr):
    P = k.P
    n = len(xin)
    P.actf(k.nln[:, :W], k.ps[pb][:, :W], AF.Ln, [k.tps[pb]], [k.tnln], scale=1.0 / nfeat, bias=EPS)
    P.actf(rs[:, :W], k.nln[:, :W], AF.Exp, [k.tnln], [trs], scale=-0.5)
    for c in range(n):
        i = k.rr % 2
        k.rr += 1
        P.stt(k.ntmp[i][:, :W], xin[c], A[c], rs[:, :W], ALU.mult, ALU.mult,
              list(rd) + [trs, k.tmods], [k.tntmp[i]])
        P.actf(outs[c], k.ntmp[i][:, :W], AF.Identity, [k.tntmp[i], k.tmods], list(wr), bias=B[c])


def phase_ffn(k, l, wi, do_ctx, final=False, src=None):
    P, dr, nc = k.P, k.dr, k.nc
    fi = l * 2 + wi
    js, jg = (0, 2) if wi == 0 else (6, 8)
    tiles = [(b * SEQ + h * 1024, 1024, b) for b in range(NB) for h in range(2)]
    if do_ctx:
        tiles.append((NLAT, 512, 2))
    xsrc = src if src is not None else k.XT
    with ExitStack() as st:
        sb = lambda n, s, d: k.sb(n, s, d, st)
        xs = [sb(f"f_x{i}", [128, 8, 1024], F32) for i in range(2)]
        hT = [sb(f"f_h{i}", [128, 8, 1024], BF16) for i in range(2)]
        hid = sb("f_hid", [128, 22, 1024], BF16)
        wbi = [sb(f"f_wbi{i}", [128, 8, 256], BF16) for i in range(3)]
        wbo = [sb(f"f_wbo{i}", [128, 22, 128], BF16) for i in range(2)]
        sg = [sb(f"f_sg{i}", [128, 512], F32) for i in range(2)]
        rs2 = [sb(f"f_rs{i}", [128, 512], F32) for i in range(2)]
        tx, th = [Tok(), Tok()], [Tok(), Tok()]
        thid = Tok()
        twbi, twbo = [Tok(), Tok(), Tok()], [Tok(), Tok()]
        tsg, trs2 = [Tok(), Tok()], [Tok(), Tok()]
        cnt = 0
        nt = len(tiles)

        def load_x(ti):
            c0, W, r = tiles[ti]
            P.dma(xs[ti % 2][:, :, :W], xsrc[:, :, c0:c0 + W], writes=[tx[ti % 2]])

        def norm_tile(ti):
            c0, W, r = tiles[ti]
            u = ti % 2
            NS = W // 512
            for s in range(NS):
                sl = slice(s * 512, (s + 1) * 512)
                norm_a(k, [xs[u][:, c, sl] for c in range(8)], 512, 6 + s, [tx[u]])
            for s in range(NS):
                sl = slice(s * 512, (s + 1) * 512)
                norm_b(k, [xs[u][:, c, sl] for c in range(8)], D, 512, 6 + s, [hT[u][:, c, sl] for c in range(8)],
                       [k.mods[:, l, js + 1, c, r:r + 1] for c in range(8)],
                       [k.mods[:, l, js, c, r:r + 1] for c in range(8)], rs2[s], trs2[s], [tx[u]], [th[u]])

        load_x(0)
        norm_tile(0)
        for ti, (c0, W, r) in enumerate(tiles):
            u = ti % 2
            NS = W // 512
            if ti + 1 < nt:
                load_x(ti + 1)
            P.dma(wbi[0][:], k.WBI[fi, 0], reads=[k.tWBI[fi][0]], writes=[twbi[0]])
            P.dma(wbi[1][:], k.WBI[fi, 1], reads=[k.tWBI[fi][1]], writes=[twbi[1]])
            for fp in range(22):
                i = fp % 3
                if fp + 2 < 22:
                    P.dma(wbi[(fp + 2) % 3][:], k.WBI[fi, fp + 2], reads=[k.tWBI[fi][fp + 2]], writes=[twbi[(fp + 2) % 3]])
                for s in range(NS):
                    sl = slice(s * 512, (s + 1) * 512)
                    pg, pu = 2 + (cnt % 2), 4 + (cnt % 2)
                    j = cnt % 2
                    cnt += 1
                    for kk in range(8):
                        P.mm(k.ps[pg][:], wbi[i][:, kk, 0:128], hT[u][:, kk, sl], kk == 0, kk == 7,
                             [twbi[i], th[u]], [k.tps[pg]])
                    for kk in range(8):
                        P.mm(k.ps[pu][:], wbi[i][:, kk, 128:256], hT[u][:, kk, sl], kk == 0, kk == 7,
                             [twbi[i], th[u]], [k.tps[pu]])
                    P.actf(sg[j][:], k.ps[pg][:], AF.Silu, [k.tps[pg]], [tsg[j]])
                    P.tt(P.dve, hid[:, fp, sl], sg[j][:], k.ps[pu][:], ALU.mult, [tsg[j], k.tps[pu]], [thid])
            P.dma(wbo[0][:], k.WBO[fi, 0], reads=[k.tWBO[fi][0]], writes=[twbo[0]])
            if ti + 1 < nt:
                norm_tile(ti + 1)
            for dc in range(8):
                i = dc % 2
                if dc + 1 < 8:
                    P.dma(wbo[1 - i][:], k.WBO[fi, dc + 1], reads=[k.tWBO[fi][dc + 1]], writes=[twbo[1 - i]])
                for s in range(NS):
                    sl = slice(s * 512, (s + 1) * 512)
                    pb = cnt % 2
                    cnt += 1
                    for kf in range(22):
                        P.mm(k.ps[pb][:], wbo[i][:, kf, :], hid[:, kf, sl], kf == 0, kf == 21,
                             [twbo[i], thid], [k.tps[pb]])
                    xa = xs[u][:, dc, sl]
                    P.stt(xa, k.ps[pb][:], k.mods[:, l, jg, dc, r:r + 1], xa, ALU.mult, ALU.add,
                          [k.tps[pb], tx[u], k.tmods], [tx[u]])
            if not final:
                P.dma(k.XT[:, :, c0:c0 + W], xs[u][:, :, :W], reads=[tx[u]], writes=[Tok()])
            else:
                for s in range(NS):
                    sl = slice(s * 512, (s + 1) * 512)
                    normT(k, [xs[u][:, c, sl] for c in range(8)], D, 512, [xs[u][:, c, sl] for c in range(8)],
                          A=[k.gains[:, 6, c:c + 1] for c in range(8)], rd=[tx[u]], wr=[tx[u]])
                P.dma(k.oT[:, :, c0:c0 + W], xs[u][:, :, :W], reads=[tx[u]], writes=[Tok()])
        P.barrier()


def phase_mix_ab(k):
    P, dr, nc = k.P, k.dr, k.nc
    l = 0
    tiles = [(b * SEQ + h * 512, b, h * 512) for b in range(NB) for h in range(4)] + [(NLAT, 2, None)]
    with ExitStack() as st:
        sb = lambda n, s, d: k.sb(n, s, d, st)
        wabb = sb("m_wab", [128, 8, 3200], BF16)
        twab = Tok()
        for kk in range(8):
            P.dma(wabb[:, kk, :], k.wbf["wab"][:, kk, :], reads=[k.twbf["wab"][kk]], writes=[twab])
        wuqb = sb("m_wuqb", [128, 2, 1536], BF16)
        wukvb = sb("m_wukvb", [128, 1024], BF16)
        tw2 = Tok()
        P.dma(wuqb[:], k.wbf["wuq"], reads=k.twbf["wuq"], writes=[tw2])
        P.dma(wukvb[:], k.wbf["wukv"], reads=k.twbf["wukv"], writes=[tw2])
        cos = sb("m_cos", [128, 2048], F32)
        sins = sb("m_sins", [128, 2048], F32)
        abv = sb("m_abv", [128, 4], F32)
        tcs = Tok()
        P.dma(cos[:], dr["cos"], writes=[tcs])
        P.dma(sins[:], dr["sins"], writes=[tcs])
        P.dma(abv[:], dr["abvec"], writes=[tcs])
        xs = sb("m_x", [128, 8, 512], F32)
        hT = sb("m_h", [128, 8, 512], BF16)
        tx, th = Tok(), Tok()
        ra = [sb(f"m_ra{i}", [128, 512], F32) for i in range(2)]
        rb = [sb(f"m_rb{i}", [128, 512], F32) for i in range(2)]
        tra, trb = [Tok(), Tok()], [Tok(), Tok()]
        oq = sb("m_oq", [128, 4, 512], BF16)
        okd = sb("m_okd", [128, 4, 512], BF16)
        ovd = sb("m_ovd", [128, 4, 512], BF16)
        oqn = sb("m_oqn", [128, 4, 512], BF16)
        oqr = sb("m_oqr", [128, 4, 512], BF16)
        okn = sb("m_okn", [128, 4, 512], BF16)
        okr = sb("m_okr", [128, 512], BF16)
        ovm = sb("m_ovm", [128, 4, 512], BF16)
        toq, tokd, tovd, toqn, toqr, tokn, tokr, tovm = [Tok() for _ in range(8)]
        cqn = sb("m_cqn", [128, 2, 512], BF16)
        ckvn = sb("m_ckvn", [128, 512], BF16)
        tcqn, tckvn = Tok(), Tok()
        st_ = {"b": 0, "r": 0, "c": 0}

        def nb():
            st_["b"] = (st_["b"] + 1) % 6
            return st_["b"]

        def evac(out, pb, tok):
            st_["c"] += 1
            P.copy(P.act if st_["c"] % 2 else P.dve, out, k.ps[pb][:], [k.tps[pb]], [tok])

        def rope(p1, p2, out, tok, t0):
            i = st_["r"] % 2
            st_["r"] += 1
            P.tt(P.dve, ra[i][:], k.ps[p1][:], cos[:, t0:t0 + 512], ALU.mult, [k.tps[p1], tcs], [tra[i]])
            P.tt(P.dve, rb[i][:], k.ps[p2][:], sins[:, t0:t0 + 512], ALU.mult, [k.tps[p2], tcs], [trb[i]])
            P.tt(P.pool, out, ra[i][:], rb[i][:], ALU.add, [tra[i], trb[i]], [tok])

        def proj8(pb, off):
            for kk in range(8):
                P.mm(k.ps[pb][:], wabb[:, kk, off:off + 128], hT[:, kk, :], kk == 0, kk == 7,
                     [twab, th], [k.tps[pb]])

        for (c0, r, t0) in tiles:
            latent = r != 2
            P.dma(xs[:], k.XT[:, :, c0:c0 + 512], writes=[tx])
            normT(k, [xs[:, c, :] for c in range(8)], D, 512, [hT[:, c, :] for c in range(8)],
                  A=[k.mods[:, l, 4, c, r:r + 1] for c in range(8)],
                  B=[k.mods[:, l, 3, c, r:r + 1] for c in range(8)], rd=[tx], wr=[th])
            for (off, offp, dst, ost, tost) in ((0, 512, k.QD, oq, toq), (1024, 1536, k.KD, okd, tokd)):
                for h in range(4):
                    p1 = nb()
                    proj8(p1, off + h * 128)
                    if latent:
                        p2 = nb()
                        proj8(p2, offp + h * 128)
                        rope(p1, p2, ost[:, h, :], tost, t0)
                    else:
                        evac(ost[:, h, :], p1, tost)
                P.dma(dst[:, :, c0:c0 + 512], ost[:], reads=[tost], writes=[Tok()], eng=P.pool)
            for s in range(4):
                p1 = nb()
                for kk in range(8):
                    P.mm(k.ps[p1][:], hT[:, kk, s * 128:(s + 1) * 128], wabb[:, kk, 2048:2560], kk == 0, kk == 7,
                         [twab, th], [k.tps[p1]])
                evac(ovd[:, s, :], p1, tovd)
            P.dma(k.VD[c0:c0 + 512, :].rearrange("(s p) e -> p s e", p=128), ovd[:], reads=[tovd], writes=[Tok()], eng=P.pool)
            pa, pb_ = nb(), nb()
            proj8(pa, 2560)
            proj8(pb_, 2688)
            normT(k, [k.ps[pa][:], k.ps[pb_][:]], 256, 512, [cqn[:, 0, :], cqn[:, 1, :]],
                  A=[abv[:, 0:1], abv[:, 1:2]], rd=[k.tps[pa], k.tps[pb_], tcs], wr=[tcqn])
            pc_ = nb()
            proj8(pc_, 2816)
            normT(k, [k.ps[pc_][:]], 128, 512, [ckvn[:]], A=[abv[:, 2:3]], rd=[k.tps[pc_], tcs], wr=[tckvn])
            p1 = nb()
            proj8(p1, 2944)
            if latent:
                p2 = nb()
                proj8(p2, 3072)
                rope(p1, p2, okr[:], tokr, t0)
            else:
                evac(okr[:], p1, tokr)
            P.dma(k.KR[:, c0:c0 + 512], okr[:], reads=[tokr], writes=[Tok()], eng=P.pool)
            for h in range(4):
                p1 = nb()
                for kk in range(2):
                    P.mm(k.ps[p1][:], wuqb[:, kk, h * 128:(h + 1) * 128], cqn[:, kk, :], kk == 0, kk == 1,
                         [tw2, tcqn], [k.tps[p1]])
                evac(oqn[:, h, :], p1, toqn)
            P.dma(k.QN[:, :, c0:c0 + 512], oqn[:], reads=[toqn], writes=[Tok()], eng=P.pool)
            for h in range(4):
                p1 = nb()
                for kk in range(2):
                    P.mm(k.ps[p1][:], wuqb[:, kk, 512 + h * 128:512 + (h + 1) * 128], cqn[:, kk, :], kk == 0, kk == 1,
                         [tw2, tcqn], [k.tps[p1]])
                if latent:
                    p2 = nb()
                    for kk in range(2):
                        P.mm(k.ps[p2][:], wuqb[:, kk, 1024 + h * 128:1024 + (h + 1) * 128], cqn[:, kk, :],
                             kk == 0, kk == 1, [tw2, tcqn], [k.tps[p2]])
                    rope(p1, p2, oqr[:, h, :], toqr, t0)
                else:
                    evac(oqr[:, h, :], p1, toqr)
            P.dma(k.QR[:, :, c0:c0 + 512], oqr[:], reads=[toqr], writes=[Tok()], eng=P.pool)
            for h in range(4):
                p1 = nb()
                P.mm(k.ps[p1][:], wukvb[:, h * 128:(h + 1) * 128], ckvn[:], True, True, [tw2, tckvn], [k.tps[p1]])
                evac(okn[:, h, :], p1, tokn)
            P.dma(k.KN[:, :, c0:c0 + 512], okn[:], reads=[tokn], writes=[Tok()], eng=P.pool)
            for s in range(4):
                p1 = nb()
                P.mm(k.ps[p1][:], ckvn[:, s * 128:(s + 1) * 128], wukvb[:, 512:1024], True, True,
                     [tw2, tckvn], [k.tps[p1]])
                evac(ovm[:, s, :], p1, tovm)
            P.dma(k.VM[c0:c0 + 512, :].rearrange("(s p) e -> p s e", p=128), ovm[:], reads=[tovm], writes=[Tok()], eng=P.pool)
        P.barrier()


def phase_att_ab(k):
    P, dr, nc = k.P, k.dr, k.nc
    with ExitStack() as st:
        sb = lambda n, s, d: k.sb(n, s, d, st)
        kdh = sb("a_kdh", [128, 2304], BF16)
        vdh = sb("a_vdh", [128, 18, 128], BF16)
        krb = sb("a_krb", [128, 2304], BF16)
        qdt = sb("a_qdt", [128, 512], BF16)
        qrt = sb("a_qrt", [128, 512], BF16)
        q1 = sb("a_q1", [128, 512], BF16)
        q2 = sb("a_q2", [128, 512], BF16)
        pt = [sb(f"a_pt{i}", [128, 512], BF16) for i in range(2)]
        r1 = sb("a_r1", [128, 512], F32)
        r2 = sb("a_r2", [128, 512], F32)
        t1 = sb("a_t1", [128, 512], F32)
        t2 = sb("a_t2", [128, 512], F32)
        oo = sb("a_oo", [128, 512], F32)
        cato = [sb(f"a_cato{i}", [128, 512], BF16) for i in range(2)]
        lv = sb("a_lv", [128, 4, 64], F32)
        ltmp = sb("a_ltmp", [128, 64], F32)
        lsc = sb("a_lsc", [128, 4], F32)
        abv = sb("a_abv", [128, 4], F32)
        gsub = sb("a_gsub", [128, 1], F32)
        tkd, tvd, tkr, tq, tqr, tq1, tq2 = [Tok() for _ in range(7)]
        tpt = [Tok(), Tok()]
        tr1, tr2, tt1, tt2, too = [Tok() for _ in range(5)]
        tcato = [Tok(), Tok()]
        tl = Tok()
        P.dma(lv[:], dr["lamv"].partition_broadcast(128), writes=[tl])
        P.dma(abv[:], dr["abvec"], writes=[tl])
        for i in range(2):
            P.op(P.dve, lambda e, i=i: e.scalar_tensor_tensor(
                out=ltmp[:], in0=lv[:, 2 * i, :], scalar=1.0, in1=lv[:, 2 * i + 1, :],
                op0=ALU.mult, op1=ALU.mult, accum_out=lsc[:, i:i + 1]), [tl], [tl])
        P.actf(lsc[:, 0:2], lsc[:, 0:2], AF.Exp, [tl], [tl])
        P.tt(P.dve, lsc[:, 2:3], lsc[:, 1:2], lsc[:, 0:1], ALU.subtract, [tl], [tl])
        P.ts(P.dve, lsc[:, 3:4], lsc[:, 2:3], -LAMBDA_INIT0, ALU.add, [tl], [tl])
        P.ts(P.dve, gsub[:], abv[:, 3:4], 1.0 - LAMBDA_INIT0, ALU.mult, [tl], [tl])
        neglam = lsc[:, 3:4]
        ptb = pt + [sb("a_pt2", [128, 512], BF16)]
        tptb = tpt + [Tok()]
        SB_ = [0, 1, 6]
        st_ = {"n": 0, "pend": None}
        eO = [sb(f"a_eO{i}", [128, 512], F32) for i in range(2)]
        eD = [sb(f"a_eD{i}", [128, 512], F32) for i in range(2)]
        teO, teD = [Tok(), Tok()], [Tok(), Tok()]

        def run_pipe(A, B, depth=2):
            n = len(A)
            for i in range(n + depth):
                if i < n:
                    A[i]()
                if i - depth >= 0:
                    B[i - depth]()
                if i == 5 and st_["pend"] is not None:
                    st_["pend"]()
                    st_["pend"] = None
            if st_["pend"] is not None and n <= 5:
                st_["pend"]()
                st_["pend"] = None

        ci = 0
        for b in range(NB):
            lat0, cx0 = b * SEQ, NLAT + b * CTX
            qtiles = [(lat0 + q * 512, 512, list(range(18))) for q in range(4)] + [(cx0, 256, [16, 17])]
            for h in range(4):
                P.dma(kdh[:, 0:2048], k.KD[:, h, lat0:lat0 + 2048], writes=[tkd])
                P.dma(kdh[:, 2048:2304], k.KD[:, h, cx0:cx0 + 256], writes=[tkd])
                P.dma(vdh[:, 0:16, :], k.VD[lat0:lat0 + 2048, h * 128:(h + 1) * 128].rearrange("(c p) e -> p c e", p=128),
                      writes=[tvd])
                P.dma(vdh[:, 16:18, :], k.VD[cx0:cx0 + 256, h * 128:(h + 1) * 128].rearrange("(c p) e -> p c e", p=128),
                      writes=[tvd])
                for (q0, W, chunks) in qtiles:
                    P.dma(qdt[:, :W], k.QD[:, h, q0:q0 + W], writes=[tq])
                    P.copy(P.dve, q1[0:64, :W], qdt[0:64, :W], [tq], [tq1])
                    P.memset(P.dve, q1[64:128, :W], 0.0, [tq1])
                    P.copy(P.dve, q2[64:128, :W], qdt[64:128, :W], [tq], [tq2])
                    P.memset(P.dve, q2[0:64, :W], 0.0, [tq2])
                    A, B = [], []
                    nch = len(chunks)
                    for comp in range(2):
                        qm, tqm = (q1, tq1) if comp == 0 else (q2, tq2)
                        for idx, kc in enumerate(chunks):
                            j = st_["n"] % 3
                            st_["n"] += 1

                            def a_(j=j, kc=kc, qm=qm, tqm=tqm, W=W):
                                pb = SB_[j]
                                P.mm(k.ps[pb][:, :W], kdh[:, kc * 128:(kc + 1) * 128], qm[:, :W], True, True,
                                     [tkd, tqm], [k.tps[pb]])
                                P.actf(ptb[j][:, :W], k.ps[pb][:, :W], AF.Exp, [k.tps[pb]], [tptb[j]], scale=0.125)

                            def b_(j=j, kc=kc, comp=comp, idx=idx, W=W, nch=nch):
                                P.mm(k.ps[2 + comp][:, :W], vdh[:, kc, :], ptb[j][:, :W], idx == 0, idx == nch - 1,
                                     [tvd, tptb[j]], [k.tps[2 + comp]])
                                P.mm(k.ps[4 + comp][:, :W], k.onesb[:], ptb[j][:, :W], idx == 0, idx == nch - 1,
                                     [k.tconst, tptb[j]], [k.tps[4 + comp]])
                            A.append(a_)
                            B.append(b_)
                    run_pipe(A, B)
                    P.copy(P.act, eO[0][:, :W], k.ps[2][:, :W], [k.tps[2]], [teO[0]])
                    P.copy(P.dve, eD[0][:, :W], k.ps[4][:, :W], [k.tps[4]], [teD[0]])
                    P.copy(P.act, eO[1][:, :W], k.ps[3][:, :W], [k.tps[3]], [teO[1]])
                    P.copy(P.dve, eD[1][:, :W], k.ps[5][:, :W], [k.tps[5]], [teD[1]])

                    def comb(W=W, h=h, q0=q0):
                        P.op(P.dve, lambda e: e.reciprocal(out=r1[:, :W], in_=eD[0][:, :W]), [teD[0]], [tr1])
                        P.op(P.dve, lambda e: e.reciprocal(out=r2[:, :W], in_=eD[1][:, :W]), [teD[1]], [tr2])
                        P.tt(P.dve, t1[:, :W], eO[0][:, :W], r1[:, :W], ALU.mult, [teO[0], tr1], [tt1])
                        P.tt(P.dve, t2[:, :W], eO[1][:, :W], r2[:, :W], ALU.mult, [teO[1], tr2], [tt2])
                        P.stt(oo[:, :W], t2[:, :W], neglam, t1[:, :W], ALU.mult, ALU.add, [tt1, tt2, tl], [too])
                        st_["c"] = st_.get("c", 0) + 1
                        co = st_["c"] % 2
                        normT(k, [oo[:, :W]], 128, W, [cato[co][:, :W]], A=[gsub[:, 0:1]], rd=[too, tl], wr=[tcato[co]])
                        P.dma(k.CAT[:, h, q0:q0 + W], cato[co][:, :W], reads=[tcato[co]], writes=[Tok()])
                    if st_["pend"] is not None:
                        st_["pend"]()
                    st_["pend"] = comb
            P.dma(krb[:, 0:2048], k.KR[:, lat0:lat0 + 2048], writes=[tkr])
            P.dma(krb[:, 2048:2304], k.KR[:, cx0:cx0 + 256], writes=[tkr])
            msc = float(192.0 ** -0.5)
            for h in range(4):
                P.dma(kdh[:, 0:2048], k.KN[:, h, lat0:lat0 + 2048], writes=[tkd])
                P.dma(kdh[:, 2048:2304], k.KN[:, h, cx0:cx0 + 256], writes=[tkd])
                P.dma(vdh[:, 0:16, :], k.VM[lat0:lat0 + 2048, h * 128:(h + 1) * 128].rearrange("(c p) e -> p c e", p=128),
                      writes=[tvd])
                P.dma(vdh[:, 16:18, :], k.VM[cx0:cx0 + 256, h * 128:(h + 1) * 128].rearrange("(c p) e -> p c e", p=128),
                      writes=[tvd])
                for (q0, W, chunks) in qtiles:
                    P.dma(qdt[:, :W], k.QN[:, h, q0:q0 + W], writes=[tq])
                    P.dma(qrt[:, :W], k.QR[:, h, q0:q0 + W], writes=[tqr])
                    A, B = [], []
                    nch = len(chunks)
                    for idx, kc in enumerate(chunks):
                        j = st_["n"] % 3
                        st_["n"] += 1

                        def a_(j=j, kc=kc, W=W):
                            pb = SB_[j]
                            P.mm(k.ps[pb][:, :W], kdh[:, kc * 128:(kc + 1) * 128], qdt[:, :W], True, False,
                                 [tkd, tq], [k.tps[pb]])
                            P.mm(k.ps[pb][:, :W], krb[:, kc * 128:(kc + 1) * 128], qrt[:, :W], False, True,
                                 [tkr, tqr], [k.tps[pb]])
                            P.actf(ptb[j][:, :W], k.ps[pb][:, :W], AF.Exp, [k.tps[pb]], [tptb[j]], scale=msc)

                        def b_(j=j, kc=kc, idx=idx, W=W, nch=nch):
                            P.mm(k.ps[2][:, :W], vdh[:, kc, :], ptb[j][:, :W], idx == 0, idx == nch - 1,
                                 [tvd, tptb[j]], [k.tps[2]])
                            P.mm(k.ps[4][:, :W], k.onesb[:], ptb[j][:, :W], idx == 0, idx == nch - 1,
                                 [k.tconst, tptb[j]], [k.tps[4]])
                        A.append(a_)
                        B.append(b_)
                    run_pipe(A, B)
                    P.copy(P.act, eO[0][:, :W], k.ps[2][:, :W], [k.tps[2]], [teO[0]])
                    P.copy(P.dve, eD[0][:, :W], k.ps[4][:, :W], [k.tps[4]], [teD[0]])

                    def comb(W=W, h=h, q0=q0):
                        P.op(P.dve, lambda e: e.reciprocal(out=r1[:, :W], in_=eD[0][:, :W]), [teD[0]], [tr1])
                        st_["c"] = st_.get("c", 0) + 1
                        co = st_["c"] % 2
                        P.tt(P.dve, cato[co][:, :W], eO[0][:, :W], r1[:, :W], ALU.mult, [teO[0], tr1], [tcato[co]])
                        P.dma(k.CAT[:, 4 + h, q0:q0 + W], cato[co][:, :W], reads=[tcato[co]], writes=[Tok()])
                    if st_["pend"] is not None:
                        st_["pend"]()
                    st_["pend"] = comb
        if st_["pend"] is not None:
            st_["pend"]()
            st_["pend"] = None
        P.barrier()


def phase_outproj(k, l, wname, nk, do_ctx):
    P, dr, nc = k.P, k.dr, k.nc
    tiles = [(b * SEQ + h * 1024, 1024, b) for b in range(NB) for h in range(2)]
    if do_ctx:
        tiles.append((NLAT, 512, 2))
    with ExitStack() as st:
        sb = lambda n, s, d: k.sb(n, s, d, st)
        xs = sb("o_x", [128, 8, 1024], F32)
        wmo = sb("o_wmo", [128, nk, 1024], BF16)
        cat = sb("o_cat", [128, nk, 1024], BF16)
        tx, twm, tcat = Tok(), Tok(), Tok()
        P.dma(wmo[:], k.wbf[wname], reads=k.twbf[wname], writes=[twm])
        cnt = 0
        for (c0, W, r) in tiles:
            NS = W // 512
            P.dma(xs[:, :, :W], k.XT[:, :, c0:c0 + W], writes=[tx])
            P.dma(cat[:, :, :W], k.CAT[:, 0:nk, c0:c0 + W], writes=[tcat])
            for dc in range(8):
                for s in range(NS):
                    pb = cnt % 4
                    cnt += 1
                    for kk in range(nk):
                        P.mm(k.ps[pb][:], wmo[:, kk, dc * 128:(dc + 1) * 128], cat[:, kk, s * 512:(s + 1) * 512],
                             kk == 0, kk == nk - 1, [twm, tcat], [k.tps[pb]])
                    xa = xs[:, dc, s * 512:(s + 1) * 512]
                    P.stt(xa, k.ps[pb][:], k.mods[:, l, 5, dc, r:r + 1], xa, ALU.mult, ALU.add,
                          [k.tps[pb], tx, k.tmods], [tx])
            P.dma(k.XT[:, :, c0:c0 + W], xs[:, :, :W], reads=[tx], writes=[Tok()])
        P.barrier()


def phase_mix_cd(k):
    P, dr, nc = k.P, k.dr, k.nc
    l = 1
    tiles = [(b * SEQ + h * 512, b) for b in range(NB) for h in range(4)] + [(NLAT, 2)]
    OZ, OX, OQ, OK_, OV, ODT = 0, 1024, 3072, 3584, 4096, 4608
    with ExitStack() as st:
        sb = lambda n, s, d: k.sb(n, s, d, st)
        wb = sb("c_wb", [128, 8, 4736], BF16)
        tw = Tok()
        for kk in range(8):
            P.dma(wb[:, kk, :], k.wbf["wcd"][:, kk, :], reads=[k.twbf["wcd"][kk]], writes=[tw])
        dtb = sb("c_dtb", [128, 1, 32], F32)
        abc = sb("c_abc", [128, 1, 32], F32)
        tcs = Tok()
        P.dma(dtb[:], dr["dtb"].partition_broadcast(128), writes=[tcs])
        P.dma(abc[:], dr["alog"].partition_broadcast(128), writes=[tcs])
        P.actf(abc[:], abc[:], AF.Exp, [tcs], [tcs])
        P.ts(P.dve, abc[:], abc[:], -1.0, ALU.mult, [tcs], [tcs])
        xs = sb("c_x", [128, 8, 512], F32)
        hT = sb("c_h", [128, 8, 512], BF16)
        tx, th = Tok(), Tok()
        osz = sb("c_osz", [128, 8, 512], F32)
        oxbc = sb("c_oxbc", [128, 16, 512], F32)
        onq = sb("c_onq", [128, 4, 512], BF16)
        onk = sb("c_onk", [128, 4, 512], BF16)
        onv = sb("c_onv", [128, 4, 512], BF16)
        tosz, toxbc, tonq, tonk, tonv = [Tok() for _ in range(5)]
        d1 = sb("c_d1", [128, 32], F32)
        d2 = sb("c_d2", [128, 32], F32)
        d3 = sb("c_d3", [128, 32], F32)
        td = Tok()
        st_ = {"b": 0, "c": 0}

        def nb():
            st_["b"] = (st_["b"] + 1) % 6
            return st_["b"]

        def evac(out, pb, tok):
            st_["c"] += 1
            P.copy(P.act if st_["c"] % 2 else P.dve, out, k.ps[pb][:], [k.tps[pb]], [tok])

        def proj8(pb, off):
            for kk in range(8):
                P.mm(k.ps[pb][:], wb[:, kk, off:off + 128], hT[:, kk, :], kk == 0, kk == 7, [tw, th], [k.tps[pb]])

        for (c0, r) in tiles:
            latent = r != 2
            P.dma(xs[:], k.XT[:, :, c0:c0 + 512], writes=[tx])
            normT(k, [xs[:, c, :] for c in range(8)], D, 512, [hT[:, c, :] for c in range(8)],
                  A=[k.mods[:, l, 4, c, r:r + 1] for c in range(8)],
                  B=[k.mods[:, l, 3, c, r:r + 1] for c in range(8)], rd=[tx], wr=[th])
            if latent:
                for j in range(8):
                    p1 = nb()
                    proj8(p1, OZ + j * 128)
                    P.actf(osz[:, j, :], k.ps[p1][:], AF.Silu, [k.tps[p1]], [tosz])
                P.dma(k.SZ[:, :, c0:c0 + 512], osz[:], reads=[tosz], writes=[Tok()], eng=P.pool)
            for j in range(16):
                p1 = nb()
                proj8(p1, OX + j * 128)
                evac(oxbc[:, j, :], p1, toxbc)
            P.dma(k.XBC[:, :, c0:c0 + 512], oxbc[:], reads=[toxbc], writes=[Tok()], eng=P.pool)
            if latent:
                for j in range(4):
                    p1 = nb()
                    proj8(p1, OQ + j * 128)
                    evac(onq[:, j, :], p1, tonq)
                P.dma(k.NQ[:, :, c0:c0 + 512], onq[:], reads=[tonq], writes=[Tok()], eng=P.pool)
            for j in range(4):
                p1 = nb()
                proj8(p1, OK_ + j * 128)
                evac(onk[:, j, :], p1, tonk)
            P.dma(k.NK[:, :, c0:c0 + 512], onk[:], reads=[tonk], writes=[Tok()], eng=P.pool)
            for s in range(4):
                p1 = nb()
                for kk in range(8):
                    P.mm(k.ps[p1][:], hT[:, kk, s * 128:(s + 1) * 128], wb[:, kk, OV:OV + 512], kk == 0, kk == 7,
                         [tw, th], [k.tps[p1]])
                evac(onv[:, s, :], p1, tonv)
            P.dma(k.NV[c0:c0 + 512, :].rearrange("(s p) e -> p s e", p=128), onv[:], reads=[tonv], writes=[Tok()], eng=P.pool)
            for s in range(4):
                q = (c0 + s * 128) // 128
                p1 = nb()
                for kk in range(8):
                    P.mm(k.ps[p1][:, 0:32], hT[:, kk, s * 128:(s + 1) * 128], wb[:, kk, ODT:ODT + 32], kk == 0, kk == 7,
                         [tw, th], [k.tps[p1]])
                P.tt(P.dve, d1[:], k.ps[p1][:, 0:32], dtb[:, 0, :], ALU.add, [k.tps[p1], tcs], [td])
                P.actf(d2[:], d1[:], AF.Abs, [td], [td])
                P.actf(d2[:], d2[:], AF.Exp, [td], [td], scale=-1.0)
                P.actf(d2[:], d2[:], AF.Ln, [td], [td], bias=1.0)
                P.ts(P.dve, d3[:], d1[:], 0.0, ALU.max, [td], [td])
                P.tt(P.dve, k.DTs[:, q, :], d3[:], d2[:], ALU.add, [td], [k.tdt])
                P.tt(P.dve, k.ADTs[:, q, :], k.DTs[:, q, :], abc[:, 0, :], ALU.mult, [k.tdt, tcs], [k.tdt])
        P.barrier()


def conv_setup(k, st):
    P, dr, nc = k.P, k.dr, k.nc
    sb = lambda n, s, d: k.sb(n, s, d, st)
    cw = sb("v_cw", [128, 16, 5], F32)
    cb = sb("v_cb", [128, 16], F32)
    tc_ = Tok()
    P.dma(cw[:], dr["convw"], writes=[tc_])
    P.dma(cb[:], dr["convb"], writes=[tc_])
    buf = [sb(f"v_buf{i}", [128, 2052], F32) for i in range(2)]
    tbuf = [Tok(), Tok()]
    for i in range(2):
        P.memset(P.dve, buf[i][:, 0:2], 0.0, [tbuf[i]])
        P.memset(P.dve, buf[i][:, 2050:2052], 0.0, [tbuf[i]])
    acc = sb("v_acc", [128, 2048], F32)
    so = sb("v_so", [128, 2048], F32)
    sob = sb("v_sob", [128, 2048], BF16)
    obt = sb("v_obt", [128, 16, 128], BF16)
    tacc, tso, tsob, tobt = Tok(), Tok(), Tok(), Tok()
    items = [(c0, W, ch) for b in range(NB) for (c0, W) in ((b * SEQ, SEQ), (NLAT + b * CTX, CTX))
             for ch in range(16)]

    def load(n):
        c0, W, ch = items[n]
        i = n % 2
        if W < SEQ:
            P.memset(P.dve, buf[i][:, 2 + W:4 + W], 0.0, [tbuf[i]])
        P.dma(buf[i][:, 2:2 + W], k.XBC[:, ch, c0:c0 + W], writes=[tbuf[i]])

    def item(n):
        c0, W, ch = items[n]
        i = n % 2
        if n == 0:
            load(0)
        if n + 1 < len(items):
            load(n + 1)
        P.actf(acc[:, :W], buf[i][:, 0:W], AF.Identity, [tbuf[i], tc_], [tacc],
               scale=cw[:, ch, 0:1], bias=cb[:, ch:ch + 1])
        yield
        for j in range(1, 5):
            P.stt(acc[:, :W], buf[i][:, j:j + W], cw[:, ch, j:j + 1], acc[:, :W], ALU.mult, ALU.add,
                  [tbuf[i], tc_, tacc], [tacc])
            if W == SEQ:
                yield
        P.actf(so[:, :W], acc[:, :W], AF.Silu, [tacc], [tso])
        if ch < 8:
            P.dma(k.XS[:, ch, c0:c0 + W], so[:, :W], reads=[tso], writes=[Tok()])
        else:
            P.copy(P.pool, sob[:, :W], so[:, :W], [tso], [tsob])
            P.dma(k.BCT[:, ch - 8, c0:c0 + W], sob[:, :W], reads=[tsob], writes=[Tok()], eng=P.pool)
        if 8 <= ch < 12:
            g = ch - 8
            nt = W // 128
            for t4 in range(0, nt, 4):
                pb = (t4 // 4) % 4
                m = min(4, nt - t4)
                for tt_ in range(m):
                    P.tr(k.ps[pb][:, tt_ * 128:(tt_ + 1) * 128], so[:, (t4 + tt_) * 128:(t4 + tt_ + 1) * 128],
                         k.ident[:], [tso, k.tconst], [k.tps[pb]])
                P.copy(P.act, obt[:, t4:t4 + m, :].rearrange("p a b -> p (a b)"), k.ps[pb][:, 0:m * 128],
                       [k.tps[pb]], [tobt])
            P.dma(k.BTOK[c0:c0 + W, g * 128:(g + 1) * 128].rearrange("(c p) n -> p c n", p=128),
                  obt[:, 0:nt, :], reads=[tobt], writes=[Tok()])
        yield

    def allsteps():
        for n in range(len(items)):
            yield from item(n)
    return allsteps()


def phase_ssd(k):
    P, dr, nc = k.P, k.dr, k.nc
    with ExitStack() as st:
        sb = lambda n, s, d: k.sb(n, s, d, st)
        masks = sb("s_masks", [128, 4, 128], F32)
        dskt = sb("s_dsk", [128, 2, 8], F32)
        dsum = sb("s_dsum", [128, 8], F32)
        gn = sb("s_gn", [128, 8], F32)
        tm = Tok()
        P.dma(masks[:], dr["masks"], writes=[tm])
        P.dma(dskt[:], dr["dsk"], writes=[tm])
        P.dma(gn[:], dr["gnorm"], writes=[tm])
        P.tt(P.dve, dsum[:], dskt[:, 0, :], dskt[:, 1, :], ALU.add, [tm], [tm])
        ACS = sb("s_acs", [128, 36, 32], F32)
        ATOT = sb("s_atot", [128, 36, 32], F32)
        EA = sb("s_ea", [128, 36, 32], F32)
        F1 = sb("s_f1", [128, 36, 32], F32)
        ETOT = sb("s_etot", [128, 36, 32], F32)
        tp = Tok()
        fl = lambda t: t[:].rearrange("p a b -> p (a b)")
        for g3 in range(3):
            q0 = g3 * 12
            for qi in range(12):
                q = q0 + qi
                P.mm(k.ps[0][:, qi * 32:qi * 32 + 16], masks[:, 0, :], k.ADTs[:, q, 0:16], True, True,
                     [tm, k.tdt], [k.tps[0]])
                P.mm(k.ps[0][:, qi * 32 + 16:qi * 32 + 32], masks[:, 1, :], k.ADTs[:, q, 16:32], True, True,
                     [tm, k.tdt], [k.tps[0]])
                P.mm(k.ps[1][:, qi * 32:qi * 32 + 32], k.ones[:], k.ADTs[:, q, :], True, True,
                     [k.tconst, k.tdt], [k.tps[1]])
            P.copy(P.dve, ACS[:, q0:q0 + 12, :].rearrange("p a b -> p (a b)"), k.ps[0][:, 0:384], [k.tps[0]], [tp])
            P.copy(P.act, ATOT[:, q0:q0 + 12, :].rearrange("p a b -> p (a b)"), k.ps[1][:, 0:384], [k.tps[1]], [tp])
        P.actf(fl(EA), fl(ACS), AF.Exp, [tp], [tp])
        P.tt(P.dve, fl(F1), fl(ATOT), fl(ACS), ALU.subtract, [tp], [tp])
        P.actf(fl(F1), fl(F1), AF.Exp, [tp], [tp])
        P.tt(P.dve, fl(F1), fl(F1), fl(k.DTs), ALU.mult, [tp, k.tdt], [tp])
        P.actf(fl(ETOT), fl(ATOT), AF.Exp, [tp], [tp])
        P.barrier()

        xsa = sb("s_xsa", [128, 2, 2304], F32)
        bta = sb("s_bta", [128, 2304], BF16)
        cta = sb("s_cta", [128, 2048], BF16)
        btk = sb("s_btk", [128, 18, 128], BF16)
        Yg = sb("s_yg", [128, 16, 256], F32)
        S = sb("s_S", [128, 256], F32)
        Sb = sb("s_Sb", [128, 256], BF16)
        class Set:
            pass
        sets = []
        for si in range(2):
            z = Set()
            z.xdd = sb(f"s_xdd{si}", [128, 256], BF16)
            z.xd = sb(f"s_xd{si}", [128, 256], BF16)
            z.Gm = sb(f"s_gm{si}", [128, 128], F32)
            z.lD = [sb(f"s_lD{si}_{i}", [128, 128], F32) for i in range(4)]
            z.E = sb(f"s_E{si}", [128, 512], F32)
            z.WT = sb(f"s_WT{si}", [128, 4, 128], BF16)
            z.yo = sb(f"s_yo{si}", [128, 256], F32)
            z.yo2 = sb(f"s_yo2{si}", [128, 256], F32)
            z.txdd, z.txd, z.tGm, z.tE, z.tWT, z.tyo, z.tyo2 = [Tok() for _ in range(7)]
            z.tlD = [Tok() for _ in range(4)]
            z.txdh = [Tok() for _ in range(4)]
            bb = 4 * si
            z.bx, z.bD, z.by, z.bs = k.ps[bb], k.ps[bb + 1], k.ps[bb + 2], k.ps[bb + 3]
            z.tx_, z.tG_, z.tD_, z.ty_, z.tyo_ = [Tok() for _ in range(5)]
            z.ts_ = k.tps[bb + 3]
            sets.append(z)
        tmpx = sb("s_tmpx", [128, 512], F32)
        szt = sb("s_szt", [128, 512], F32)
        yz = [sb(f"s_yz{i}", [128, 512], F32) for i in range(2)]
        ocat = sb("s_ocat", [128, 2, 512], BF16)
        tin, tY, tS, tSb = [Tok() for _ in range(4)]
        ttmpx, tszt, tocat = Tok(), Tok(), Tok()
        tyz = [Tok(), Tok()]
        h3 = lambda ap: ap.rearrange("p (h e) -> p h e", e=64)
        bc = lambda ap: ap.unsqueeze(2).to_broadcast([128, 4, 64])
        nx = 0
        for b in range(NB):
            lat0, cx0 = b * SEQ, NLAT + b * CTX
            for g in range(4):
                P.dma(xsa[:, :, 0:2048], k.XS[:, 2 * g:2 * g + 2, lat0:lat0 + 2048], writes=[tin])
                P.dma(xsa[:, :, 2048:2304], k.XS[:, 2 * g:2 * g + 2, cx0:cx0 + 256], writes=[tin])
                P.dma(bta[:, 0:2048], k.BCT[:, g, lat0:lat0 + 2048], writes=[tin])
                P.dma(bta[:, 2048:2304], k.BCT[:, g, cx0:cx0 + 256], writes=[tin])
                P.dma(cta[:], k.BCT[:, 4 + g, lat0:lat0 + 2048], writes=[tin])
                P.dma(btk[:, 0:16, :], k.BTOK[lat0:lat0 + 2048, g * 128:(g + 1) * 128].rearrange("(c p) n -> p c n", p=128),
                      writes=[tin])
                P.dma(btk[:, 16:18, :], k.BTOK[cx0:cx0 + 256, g * 128:(g + 1) * 128].rearrange("(c p) n -> p c n", p=128),
                      writes=[tin])
                for d in range(2):
                    order = ([16, 17] + list(range(16))) if d == 0 else ([17, 16] + list(range(15, -1, -1)))
                    P.memset(P.dve, S[:], 0.0, [tS])
                    P.memset(P.pool, Sb[:], 0.0, [tSb])
                    mR = masks[:, 0, :] if d == 0 else masks[:, 1, :]
                    mS = masks[:, 3, :] if d == 0 else masks[:, 2, :]
                    S1, S2 = [], []
                    for ci, cc in enumerate(order):
                        z = sets[nx % 2]
                        nx += 1

                        def s1(ci=ci, cc=cc, z=z, d=d, mR=mR, mS=mS):
                            lat = cc < 16
                            q = (b * 16 + cc) if lat else (32 + b * 2 + (cc - 16))
                            csl = slice(cc * 128, (cc + 1) * 128)
                            h0 = d * 16 + g * 4
                            hsl = slice(h0, h0 + 4)
                            P.tr(z.bx[:, 0:128], xsa[:, 0, csl], k.ident[:], [tin, k.tconst], [z.tx_, z.tG_])
                            P.tr(z.bx[:, 128:256], xsa[:, 1, csl], k.ident[:], [tin, k.tconst], [z.tx_, z.tG_])
                            if lat:
                                P.mm(z.bx[:, 256:384], bta[:, csl], cta[:, csl], True, True, [tin], [z.tG_])
                            P.tt(P.dve, h3(z.xdd[:]), h3(z.bx[:, 0:256]), bc(F1[:, q, hsl]), ALU.mult,
                                 [z.tx_, z.tG_, tp], [z.txdd])
                            if lat:
                                P.tt(P.dve, h3(z.xd[:]), h3(z.bx[:, 0:256]), bc(k.DTs[:, q, hsl]), ALU.mult,
                                     [z.tx_, z.tG_, k.tdt], [z.txd])
                                P.tt(P.dve, z.Gm[:], z.bx[:, 256:384], mR, ALU.mult, [z.tG_, tm], [z.tGm])
                                for hh in range(4):
                                    P.actf(z.lD[hh][:], mS, AF.Identity, [tm, k.tdt], [z.tlD[hh]],
                                           scale=k.ADTs[:, q, h0 + hh:h0 + hh + 1])
                                for hh in range(4):
                                    P.mm(z.bD[:, hh * 128:(hh + 1) * 128], z.lD[hh][:], mR, True, True,
                                         [z.tlD[hh], tm], [z.tD_])
                                P.actf(z.E[:], z.bD[:], AF.Exp, [z.tD_], [z.tE])
                                P.tt(P.pool, z.WT[:], z.E[:].rearrange("p (h s) -> p h s", s=128),
                                     z.Gm[:].unsqueeze(1).to_broadcast([128, 4, 128]), ALU.mult, [z.tE, z.tGm], [z.tWT])

                        def s2(ci=ci, cc=cc, z=z, d=d):
                            lat = cc < 16
                            q = (b * 16 + cc) if lat else (32 + b * 2 + (cc - 16))
                            csl = slice(cc * 128, (cc + 1) * 128)
                            h0 = d * 16 + g * 4
                            hsl = slice(h0, h0 + 4)
                            if lat:
                                for hh in range(4):
                                    P.mm(z.by[:, hh * 64:(hh + 1) * 64], z.WT[:, hh, :], z.xd[:, hh * 64:(hh + 1) * 64],
                                         True, True, [z.tWT, z.txd], [z.ty_])
                            if ci < 17:
                                P.mm(z.bs[:, 0:256], btk[:, cc, :], z.xdd[:], True, True, [tin, z.txdd], [z.ts_])
                            if lat:
                                P.mm(z.by[:, 256:512], cta[:, csl], Sb[:], True, True, [tin, tSb], [z.tyo_])
                            if ci < 17:
                                P.tt(P.dve, h3(S[:]), h3(S[:]), bc(ETOT[:, q, hsl]), ALU.mult, [tS, tp], [tS])
                                P.tt(P.dve, S[:], S[:], z.bs[:, 0:256], ALU.add, [tS, z.ts_], [tS])
                            if lat:
                                P.tt(P.dve, h3(z.yo[:]), h3(z.by[:, 256:512]), bc(EA[:, q, hsl]), ALU.mult,
                                     [z.tyo_, tp], [z.tyo])
                            if ci < 17:
                                P.copy(P.dve, Sb[:], S[:], [tS], [tSb])
                            if lat:
                                if d == 0:
                                    P.tt(P.dve, Yg[:, cc, :], z.yo[:], z.by[:, 0:256], ALU.add, [z.tyo, z.ty_], [tY])
                                else:
                                    P.tt(P.dve, z.yo2[:], z.yo[:], z.by[:, 0:256], ALU.add, [z.tyo, z.ty_], [z.tyo2])
                                    P.tt(P.pool, Yg[:, cc, :], Yg[:, cc, :], z.yo2[:], ALU.add, [z.tyo2, tY], [tY])
                        S1.append(s1)
                        S2.append(s2)
                    for i_ in range(len(S1) + 1):
                        if i_ < len(S1):
                            S1[i_]()
                        if i_ >= 1:
                            S2[i_ - 1]()
                for qt in range(4):
                    for j in range(2):
                        pb = 2 + j
                        ptk = [sets[0].ty_, sets[0].tyo_] if j == 0 else [sets[0].ts_]
                        for t4 in range(4):
                            P.tr(k.ps[pb][:, t4 * 128:(t4 + 1) * 128], Yg[:, qt * 4 + t4, j * 128:(j + 1) * 128],
                                 k.ident[:], [tY, k.tconst], ptk)
                        P.stt(tmpx[:], xsa[:, j, qt * 512:(qt + 1) * 512], dsum[:, 2 * g + j:2 * g + j + 1],
                              k.ps[pb][:], ALU.mult, ALU.add, [tin, tm] + ptk, [ttmpx])
                        P.dma(szt[:], k.SZ[:, 2 * g + j, lat0 + qt * 512:lat0 + (qt + 1) * 512], writes=[tszt])
                        P.tt(P.dve, yz[j][:], tmpx[:], szt[:], ALU.mult, [ttmpx, tszt], [tyz[j]])
                    normT(k, [yz[0][:], yz[1][:]], 256, 512, [ocat[:, 0, :], ocat[:, 1, :]],
                          A=[gn[:, 2 * g:2 * g + 1], gn[:, 2 * g + 1:2 * g + 2]], rd=[tyz[0], tyz[1], tm], wr=[tocat])
                    P.dma(k.CAT[:, 2 * g:2 * g + 2, lat0 + qt * 512:lat0 + (qt + 1) * 512], ocat[:],
                          reads=[tocat], writes=[Tok()])
        P.barrier()


def phase_na(k):
    P, dr, nc = k.P, k.dr, k.nc
    with ExitStack() as st:
        sb = lambda n, s, d: k.sb(n, s, d, st)
        nkb = sb("n_nk", [128, 2304], BF16)
        nvp = sb("n_nv", [128, 18, 128], BF16)
        nqb = sb("n_nq", [128, 2048], BF16)
        qm = sb("n_qm", [128, 2048], BF16)
        tab = sb("n_tab", [128, 25, 128], F32)
        sa = [sb(f"n_sa{i}", [128, 512], F32) for i in range(2)]
        s4 = [sb(f"n_s4{i}", [128, 128], F32) for i in range(2)]
        pA = [sb(f"n_pA{i}", [128, 512], BF16) for i in range(2)]
        pB = [sb(f"n_pB{i}", [128, 384], BF16) for i in range(2)]
        rc = sb("n_rc", [128, 512], F32)
        ocat = sb("n_ocat", [128, 2048], BF16)
        tin, tq, tqm, ttab, trc, tocat = [Tok() for _ in range(6)]
        tsa, ts4, tpA, tpB = [Tok(), Tok()], [Tok(), Tok()], [Tok(), Tok()], [Tok(), Tok()]
        cv = conv_setup(k, st)
        cvs = {"done": False}

        def conv_next(nsteps=1):
            for _ in range(nsteps):
                if cvs["done"]:
                    return
                try:
                    next(cv)
                except StopIteration:
                    cvs["done"] = True

        n = 0
        for b in range(NB):
            lat0, cx0 = b * SEQ, NLAT + b * CTX
            for hp in range(4):
                P.dma(nkb[:, 0:2048], k.NK[:, hp, lat0:lat0 + 2048], writes=[tin])
                P.dma(nkb[:, 2048:2304], k.NK[:, hp, cx0:cx0 + 256], writes=[tin])
                P.dma(nvp[:, 0:16, :], k.NV[lat0:lat0 + 2048, hp * 128:(hp + 1) * 128].rearrange("(c p) e -> p c e", p=128),
                      writes=[tin])
                P.dma(nvp[:, 16:18, :], k.NV[cx0:cx0 + 256, hp * 128:(hp + 1) * 128].rearrange("(c p) e -> p c e", p=128),
                      writes=[tin])
                P.dma(nqb[:], k.NQ[:, hp, lat0:lat0 + 2048], writes=[tq])
                for half in range(2):
                    h = 2 * hp + half
                    hs = slice(half * 64, half * 64 + 64)
                    os_ = slice((1 - half) * 64, (1 - half) * 64 + 64)
                    P.dma(tab[:], dr["nab"][h], writes=[ttab])
                    P.copy(P.dve, qm[hs, :], nqb[hs, :], [tq], [tqm])
                    P.memset(P.dve, qm[os_, :], 0.0, [tqm])
                    A, B = [], []
                    for qb in range(16):
                        i = n % 2
                        n += 1

                        def a_(qb=qb, i=i):
                            start = min(max(qb - 2, 0), 11)
                            cls = {0: 0, 1: 1, 14: 3, 15: 4}.get(qb, 2)
                            bA, bB = i, 2 + i
                            qsl = slice(qb * 128, (qb + 1) * 128)
                            for c5 in range(5):
                                dst = k.ps[bA][:, c5 * 128:(c5 + 1) * 128] if c5 < 4 else k.ps[bB][:, 0:128]
                                tk_ = k.tps[bA] if c5 < 4 else k.tps[bB]
                                P.mm(dst, nkb[:, (start + c5) * 128:(start + c5 + 1) * 128], qm[:, qsl], True, True,
                                     [tin, tqm], [tk_])
                            for cx in range(2):
                                P.mm(k.ps[bB][:, 128 + cx * 128:256 + cx * 128], nkb[:, 2048 + cx * 128:2176 + cx * 128],
                                     qm[:, qsl], True, True, [tin, tqm], [k.tps[bB]])
                            P.stt(sa[i][:], k.ps[bA][:], 0.125, tab[:, cls * 5:cls * 5 + 4, :].rearrange("p a b -> p (a b)"),
                                  ALU.mult, ALU.add, [k.tps[bA], ttab], [tsa[i]])
                            P.stt(s4[i][:], k.ps[bB][:, 0:128], 0.125, tab[:, cls * 5 + 4, :], ALU.mult, ALU.add,
                                  [k.tps[bB], ttab], [ts4[i]])
                            P.actf(pA[i][:], sa[i][:], AF.Exp, [tsa[i]], [tpA[i]])
                            P.actf(pB[i][:, 0:128], s4[i][:], AF.Exp, [ts4[i]], [tpB[i]])
                            P.actf(pB[i][:, 128:384], k.ps[bB][:, 128:384], AF.Exp, [k.tps[bB]], [tpB[i]], scale=0.125)

                        def b_(qb=qb, i=i, hs=hs):
                            start = min(max(qb - 2, 0), 11)
                            po, pd = (4, 5) if (qb // 4) % 2 == 0 else (6, 7)
                            osl = slice((qb % 4) * 128, (qb % 4 + 1) * 128)
                            for c7 in range(7):
                                if c7 < 4:
                                    rhs, tr_ = pA[i][:, c7 * 128:(c7 + 1) * 128], tpA[i]
                                    kc = start + c7
                                elif c7 == 4:
                                    rhs, tr_ = pB[i][:, 0:128], tpB[i]
                                    kc = start + 4
                                else:
                                    rhs, tr_ = pB[i][:, 128 + (c7 - 5) * 128:256 + (c7 - 5) * 128], tpB[i]
                                    kc = 16 + (c7 - 5)
                                P.mm(k.ps[po][:, osl], nvp[:, kc, :], rhs, c7 == 0, c7 == 6, [tin, tr_], [k.tps[po]])
                                P.mm(k.ps[pd][:, osl], k.onesb[:], rhs, c7 == 0, c7 == 6, [k.tconst, tr_], [k.tps[pd]])
                            if qb % 4 == 3:
                                o0 = (qb // 4) * 512
                                P.actf(rc[hs, :], k.ps[pd][hs, :], AF.Ln, [k.tps[pd]], [trc])
                                P.actf(rc[hs, :], rc[hs, :], AF.Exp, [trc], [trc], scale=-1.0)
                                P.tt(P.dve, ocat[hs, o0:o0 + 512], k.ps[po][hs, :], rc[hs, :], ALU.mult,
                                     [k.tps[po], trc], [tocat])
                        A.append(a_)
                        B.append(b_)
                    for qi in range(17):
                        if qi < 16:
                            A[qi]()
                        if qi >= 1:
                            B[qi - 1]()
                        conv_next(2 if qi % 2 else 1)
                P.dma(k.CAT[:, 8 + hp, lat0:lat0 + 2048], ocat[:], reads=[tocat], writes=[Tok()])
        while not cvs["done"]:
            conv_next()
        P.barrier()


_CACHE = {}


def kernel(**inputs):
    sh = prep_shared(inputs)
    cores = [prep_core(inputs, c) for c in range(DBG["ncores"])]
    shapes = {n: a.shape for n, a in sh.items()}
    shapes.update({n: a.shape for n, a in cores[0].items()})
    nc = build(shapes)
    in_maps = []
    ncr = DBG["ncores"]
    for c in range(ncr):
        m = dict(sh)
        m.update(cores[c])
        in_maps.append(m)
    res = run_bass_kernel_spmd(nc, in_maps, core_ids=list(range(ncr)))
    if DBG["stop"] is not None:
        return [r["XTd"] for r in res.results]
    outs = []
    for c in range(NCORES):
        o = res.results[c]["oT"]
        o = o.transpose(2, 1, 0).reshape(NB, SEQ, D)
        outs.append(o)
    return np.ascontiguousarray(np.concatenate(outs, 0).astype(np.float32))
```
